# Optimizing a Trainium2 kernel written in Bass

```python
import jax
import jax.numpy as jnp
from jax import lax
import numpy as np


D_MODEL = 2048
BATCH = 8
SEQ = 2048
DEPTH = 2

NSA_HEADS = 16
NSA_KV_GROUPS = 4
HEADS_PER_GROUP = NSA_HEADS // NSA_KV_GROUPS
HEAD_DIM = 128
NSA_WIDTH = NSA_HEADS * HEAD_DIM
KV_WIDTH = NSA_KV_GROUPS * HEAD_DIM
CMP_LEN = 32
CMP_STRIDE = 16
SLC_LEN = 64
SLC_TOP_N = 16
SLC_LOCAL = 2
FORCE_BONUS = 1e4
WINDOW = 512
WIN_Q_BLOCK = 128
SLC_Q_BLOCK = 32

LRU_WIDTH = 2048
LRU_BLOCKS = 16
LRU_BLOCK_DIM = LRU_WIDTH // LRU_BLOCKS
LRU_C = 8.0
LRU_CONV = 4

D_FF = 3 * D_MODEL
FFN_CONV = 3

IN_WIDTH = NSA_WIDTH + 6 * KV_WIDTH + 3 * NSA_HEADS + 2 * LRU_WIDTH + 2 * D_MODEL
NORM_EPS = 1e-6

kernel_name = "nsa_rglru_gated_hybrid_block"


def rmsnorm(x, g):
    x32 = x.astype(jnp.float32)
    inv = lax.rsqrt(jnp.mean(x32 * x32, axis=-1, keepdims=True) + NORM_EPS)
    return (x32 * inv).astype(x.dtype) * g


def causal_dwconv(x, w, b):
    k = w.shape[0]
    t = x.shape[1]
    xp = jnp.pad(x, ((0, 0), (k - 1, 0), (0, 0)))
    out = b
    for j in range(k):
        out = out + xp[:, j:j + t] * w[j]
    return out


def masked_softmax(s, mask):
    s = jnp.where(mask, s.astype(jnp.float32), -jnp.inf)
    m = jnp.max(s, axis=-1, keepdims=True)
    m = jnp.where(jnp.isfinite(m), m, 0.0)
    p = jnp.exp(s - m)
    return p / jnp.maximum(jnp.sum(p, axis=-1, keepdims=True), 1e-30)


def _split_points():
    sizes = (NSA_WIDTH,) + (KV_WIDTH,) * 6 + (3 * NSA_HEADS, LRU_WIDTH, LRU_WIDTH, D_MODEL, D_MODEL)
    return [int(v) for v in np.cumsum(sizes)[:-1]]


def compress_blocks(kv, pos, w1, w2):
    t = kv.shape[1]
    n_cmp = (t - CMP_LEN) // CMP_STRIDE + 1
    idx = jnp.arange(n_cmp)[:, None] * CMP_STRIDE + jnp.arange(CMP_LEN)[None, :]
    blocks = kv[:, idx] + pos[None, None, :, None, :]
    hid = jax.nn.gelu(jnp.einsum('bnlgd,lde->bnge', blocks, w1))
    return jnp.einsum('bnge,ef->bngf', hid, w2)


def selected_attention(q, k, v, sel):
    bn, t, g, hg, hd = q.shape
    n_slc = t // SLC_LEN
    n_top = sel.shape[-1]
    nc = t // SLC_Q_BLOCK
    kb = k.reshape(bn, n_slc, SLC_LEN, g, hd).transpose(0, 3, 1, 2, 4)
    vb = v.reshape(bn, n_slc, SLC_LEN, g, hd).transpose(0, 3, 1, 2, 4)
    b_ix = jnp.arange(bn)[:, None, None, None]
    g_ix = jnp.arange(g)[None, :, None, None]
    off = jnp.arange(SLC_LEN)
    m_tok = n_top * SLC_LEN

    def chunk(args):
        q_c, sel_c, start = args
        k_g = kb[b_ix, g_ix, sel_c].reshape(bn, g, SLC_Q_BLOCK, m_tok, hd)
        v_g = vb[b_ix, g_ix, sel_c].reshape(bn, g, SLC_Q_BLOCK, m_tok, hd)
        key_pos = (sel_c[..., None] * SLC_LEN + off).reshape(bn, g, SLC_Q_BLOCK, m_tok)
        q_pos = start + jnp.arange(SLC_Q_BLOCK)
        mask = key_pos <= q_pos[None, None, :, None]
        s = jnp.einsum('bqghd,bgqmd->bghqm', q_c, k_g)
        p = masked_softmax(s, mask[:, :, None])
        return jnp.einsum('bghqm,bgqmd->bqghd', p.astype(v_g.dtype), v_g)

    q_chunks = jnp.moveaxis(q.reshape(bn, nc, SLC_Q_BLOCK, g, hg, hd), 1, 0)
    sel_chunks = jnp.moveaxis(sel.reshape(bn, g, nc, SLC_Q_BLOCK, n_top), 2, 0)
    starts = jnp.arange(nc) * SLC_Q_BLOCK
    out = lax.map(chunk, (q_chunks, sel_chunks, starts))
    return jnp.moveaxis(out, 0, 1).reshape(bn, t, g, hg, hd)


def window_attention(q, k, v):
    bn, t, g, hg, hd = q.shape
    nb = t // WIN_Q_BLOCK
    n_prev = WINDOW // WIN_Q_BLOCK

    def band(z):
        zp = jnp.pad(z, ((0, 0), (WINDOW, 0), (0, 0), (0, 0)))
        zb = zp.reshape(bn, nb + n_prev, WIN_Q_BLOCK, g, hd)
        return jnp.concatenate([zb[:, j:j + nb] for j in range(n_prev + 1)], axis=2)

    kb, vb = band(k), band(v)
    qb = q.reshape(bn, nb, WIN_Q_BLOCK, g, hg, hd)
    q_pos = jnp.arange(t).reshape(nb, WIN_Q_BLOCK)
    k_pos = q_pos[:, :1] - WINDOW + jnp.arange((n_prev + 1) * WIN_Q_BLOCK)[None, :]
    diff = q_pos[:, :, None] - k_pos[:, None, :]
    mask = (diff >= 0) & (diff < WINDOW) & (k_pos[:, None, :] >= 0)
    s = jnp.einsum('bnqghd,bnmgd->bnghqm', qb, kb)
    p = masked_softmax(s, mask[None, :, None, None])
    o = jnp.einsum('bnghqm,bnmgd->bnqghd', p.astype(vb.dtype), vb)
    return o.reshape(bn, t, g, hg, hd)


def nsa_attention(q, kc, vc, ks, vs, kw, vw, gates, pos_k, w1_k, w2_k, pos_v, w1_v, w2_v):
    bn, t, _ = q.shape
    g, hg, hd = NSA_KV_GROUPS, HEADS_PER_GROUP, HEAD_DIM
    q = q.reshape(bn, t, g, hg, hd) * (hd ** -0.5)
    kc, vc = kc.reshape(bn, t, g, hd), vc.reshape(bn, t, g, hd)
    ks, vs = ks.reshape(bn, t, g, hd), vs.reshape(bn, t, g, hd)
    kw, vw = kw.reshape(bn, t, g, hd), vw.reshape(bn, t, g, hd)
    t_pos = jnp.arange(t)

    k_cmp = compress_blocks(kc, pos_k, w1_k, w2_k)
    v_cmp = compress_blocks(vc, pos_v, w1_v, w2_v)
    n_cmp = k_cmp.shape[1]
    cmp_start = jnp.arange(n_cmp) * CMP_STRIDE
    cmp_mask = (cmp_start + CMP_LEN - 1)[None, :] <= t_pos[:, None]
    s_cmp = jnp.einsum('btghd,bngd->bghtn', q, k_cmp)
    p_cmp = masked_softmax(s_cmp, cmp_mask)
    o_cmp = jnp.einsum('bghtn,bngd->btghd', p_cmp.astype(v_cmp.dtype), v_cmp)

    n_slc = t // SLC_LEN
    slc_start = jnp.arange(n_slc) * SLC_LEN
    overlap = ((cmp_start[:, None] < slc_start[None, :] + SLC_LEN)
               & (cmp_start[:, None] + CMP_LEN > slc_start[None, :])).astype(jnp.float32)
    imp = jnp.einsum('bghtn,nj->bgtj', p_cmp, overlap)
    blk = jnp.arange(n_slc)[None, :]
    cur = (t_pos // SLC_LEN)[:, None]
    valid = blk <= cur
    forced = (blk == 0) | (valid & (blk > cur - SLC_LOCAL))
    score = jnp.where(forced, imp + FORCE_BONUS, jnp.where(valid, imp, -1.0))
    n_top = min(SLC_TOP_N, n_slc)
    _, sel = lax.top_k(score, n_top)
    o_slc = selected_attention(q, ks, vs, sel)

    o_win = window_attention(q, kw, vw)

    gt = jax.nn.sigmoid(gates.reshape(bn, t, g, hg, 3))
    o = gt[..., 0:1] * o_cmp + gt[..., 1:2] * o_slc + gt[..., 2:3] * o_win
    return o.reshape(bn, t, NSA_WIDTH)


def rg_lru(x, w_a, b_a, w_i, b_i, lam):
    bn, t, w = x.shape
    xb = x.reshape(bn, t, LRU_BLOCKS, LRU_BLOCK_DIM)
    r = jax.nn.sigmoid(jnp.einsum('btnd,nde->btne', xb, w_a).reshape(bn, t, w) + b_a)
    i = jax.nn.sigmoid(jnp.einsum('btnd,nde->btne', xb, w_i).reshape(bn, t, w) + b_i)
    log_a = (LRU_C * r.astype(jnp.float32)) * jax.nn.log_sigmoid(lam.astype(jnp.float32))
    a = jnp.exp(log_a)
    b = jnp.sqrt(-jnp.expm1(2.0 * log_a)) * (i * x).astype(jnp.float32)

    def combine(left, right):
        a1, b1 = left
        a2, b2 = right
        return a1 * a2, a2 * b1 + b2

    _, h = lax.associative_scan(combine, (a, b), axis=1)
    return h.astype(x.dtype)


def token_mixer(h, w_in, pos_k, w1_k, w2_k, pos_v, w1_v, w2_v, conv_w, conv_b,
                w_a, b_a, w_i, b_i, lam, proj_a, proj_b, w_out):
    z = h @ w_in
    q, kc, vc, ks, vs, kw, vw, g_nsa, lx, ly, ga, gb = jnp.split(z, _split_points(), axis=-1)
    o_a = nsa_attention(q, kc, vc, ks, vs, kw, vw, g_nsa, pos_k, w1_k, w2_k, pos_v, w1_v, w2_v)
    u = causal_dwconv(lx, conv_w, conv_b)
    o_b = jax.nn.gelu(ly) * rg_lru(u, w_a, b_a, w_i, b_i, lam)
    merged = jax.nn.sigmoid(ga) * (o_a @ proj_a) + jax.nn.sigmoid(gb) * (o_b @ proj_b)
    return merged @ w_out


def conv_glu_ffn(h, w_up, conv_w, conv_b, w_down):
    gate, val = jnp.split(h @ w_up, 2, axis=-1)
    gate = causal_dwconv(gate, conv_w, conv_b)
    return (jax.nn.gelu(gate) * val) @ w_down


def setup_inputs(seed: int = 0) -> dict:
    key = jax.random.key(seed)
    ks = iter(jax.random.split(key, 40))

    def nrm(shape, fan_in, scale=1.0):
        return jax.random.normal(next(ks), shape, jnp.float32) * (scale * fan_in ** -0.5)

    def small(shape, s=0.02):
        return jax.random.normal(next(ks), shape, jnp.float32) * s

    x = jax.random.normal(next(ks), (BATCH, SEQ, D_MODEL), jnp.float32)
    c = jax.random.normal(next(ks), (BATCH, D_MODEL), jnp.float32)
    ada_w = nrm((DEPTH, D_MODEL, 6 * D_MODEL), D_MODEL, 0.5)
    ada_b = small((DEPTH, 6 * D_MODEL))
    norm1_g = 1.0 + small((DEPTH, D_MODEL))
    w_in = nrm((DEPTH, D_MODEL, IN_WIDTH), D_MODEL)
    cmp_pos_k = small((DEPTH, CMP_LEN, HEAD_DIM), 0.1)
    cmp_w1_k = nrm((DEPTH, CMP_LEN, HEAD_DIM, HEAD_DIM), CMP_LEN * HEAD_DIM)
    cmp_w2_k = nrm((DEPTH, HEAD_DIM, HEAD_DIM), HEAD_DIM)
    cmp_pos_v = small((DEPTH, CMP_LEN, HEAD_DIM), 0.1)
    cmp_w1_v = nrm((DEPTH, CMP_LEN, HEAD_DIM, HEAD_DIM), CMP_LEN * HEAD_DIM)
    cmp_w2_v = nrm((DEPTH, HEAD_DIM, HEAD_DIM), HEAD_DIM)
    lru_conv_w = nrm((DEPTH, LRU_CONV, LRU_WIDTH), LRU_CONV)
    lru_conv_b = small((DEPTH, LRU_WIDTH))
    lru_wa = nrm((DEPTH, LRU_BLOCKS, LRU_BLOCK_DIM, LRU_BLOCK_DIM), LRU_BLOCK_DIM)
    lru_ba = small((DEPTH, LRU_WIDTH))
    lru_wi = nrm((DEPTH, LRU_BLOCKS, LRU_BLOCK_DIM, LRU_BLOCK_DIM), LRU_BLOCK_DIM)
    lru_bi = small((DEPTH, LRU_WIDTH))
    u = jax.random.uniform(next(ks), (DEPTH, LRU_WIDTH), jnp.float32)
    rad = jnp.sqrt(u * (0.999 ** 2 - 0.9 ** 2) + 0.9 ** 2)
    lru_lambda = jnp.log(rad) - jnp.log1p(-rad)
    proj_a = nrm((DEPTH, NSA_WIDTH, D_MODEL), NSA_WIDTH)
    proj_b = nrm((DEPTH, LRU_WIDTH, D_MODEL), LRU_WIDTH)
    w_out = nrm((DEPTH, D_MODEL, D_MODEL), D_MODEL)
    norm2_g = 1.0 + small((DEPTH, D_MODEL))
    ffn_up = nrm((DEPTH, D_MODEL, 2 * D_FF), D_MODEL)
    ffn_conv_w = nrm((DEPTH, FFN_CONV, D_FF), FFN_CONV)
    ffn_conv_b = small((DEPTH, D_FF))
    ffn_down = nrm((DEPTH, D_FF, D_MODEL), D_FF)
    final_g = 1.0 + small((D_MODEL,))
    return {"x": x, "c": c, "ada_w": ada_w, "ada_b": ada_b, "norm1_g": norm1_g, "w_in": w_in,
            "cmp_pos_k": cmp_pos_k, "cmp_w1_k": cmp_w1_k, "cmp_w2_k": cmp_w2_k,
            "cmp_pos_v": cmp_pos_v, "cmp_w1_v": cmp_w1_v, "cmp_w2_v": cmp_w2_v,
            "lru_conv_w": lru_conv_w, "lru_conv_b": lru_conv_b, "lru_wa": lru_wa, "lru_ba": lru_ba,
            "lru_wi": lru_wi, "lru_bi": lru_bi, "lru_lambda": lru_lambda,
            "proj_a": proj_a, "proj_b": proj_b, "w_out": w_out, "norm2_g": norm2_g,
            "ffn_up": ffn_up, "ffn_conv_w": ffn_conv_w, "ffn_conv_b": ffn_conv_b, "ffn_down": ffn_down,
            "final_g": final_g}


def reference(x, c, ada_w, ada_b, norm1_g, w_in, cmp_pos_k, cmp_w1_k, cmp_w2_k, cmp_pos_v, cmp_w1_v,
              cmp_w2_v, lru_conv_w, lru_conv_b, lru_wa, lru_ba, lru_wi, lru_bi, lru_lambda, proj_a, proj_b,
              w_out, norm2_g, ffn_up, ffn_conv_w, ffn_conv_b, ffn_down, final_g):
    c_act = jax.nn.silu(c)
    for l in range(DEPTH):
        mod = (c_act @ ada_w[l] + ada_b[l])[:, None, :]
        sh1, sc1, g1, sh2, sc2, g2 = jnp.split(mod, 6, axis=-1)
        h = rmsnorm(x, norm1_g[l]) * (1.0 + sc1) + sh1
        y = token_mixer(h, w_in[l], cmp_pos_k[l], cmp_w1_k[l], cmp_w2_k[l], cmp_pos_v[l], cmp_w1_v[l],
                        cmp_w2_v[l], lru_conv_w[l], lru_conv_b[l], lru_wa[l], lru_ba[l], lru_wi[l],
                        lru_bi[l], lru_lambda[l], proj_a[l], proj_b[l], w_out[l])
        x = x + g1 * y
        h = rmsnorm(x, norm2_g[l]) * (1.0 + sc2) + sh2
        x = x + g2 * conv_glu_ffn(h, ffn_up[l], ffn_conv_w[l], ffn_conv_b[l], ffn_down[l])
    return rmsnorm(x, final_g)
```

```python
import numpy as np
from contextlib import ExitStack
import concourse.bass as bass
import concourse.mybir as mybir
from concourse.bass_utils import run_bass_kernel_spmd

F32 = mybir.dt.float32
BF16 = mybir.dt.bfloat16
AF = mybir.ActivationFunctionType
ALU = mybir.AluOpType
AX = mybir.AxisListType

T = 2048
D = 2048
KC = 16
DFF = 6144
INW = 13360
NEG = -30000.0

CP = {}
_o = 0
for _n, _w in (("adab", 96), ("n1g", 16), ("n2g", 16), ("lcw", 64), ("lcb", 16), ("lba", 16), ("lbi", 16),
               ("llam", 16), ("fcw", 144), ("fcb", 48), ("posk", 32), ("posv", 32)):
    CP[_n] = (_o, _w)
    _o += _w
NCP = _o
CS = {}
_o = 0
for _n, _w in (("cT", 16), ("fing", 16), ("rowv", 16), ("cmpb", 2048), ("winb", 640), ("validm", 512),
               ("addc", 512), ("ovl", 32), ("ident", 128)):
    CS[_n] = (_o, _w)
    _o += _w
NCS = _o


class Buf:
    __slots__ = ("writer", "readers")

    def __init__(self):
        self.writer = None
        self.readers = {}


class StopBuild(Exception):
    pass


class Prog:
    stop_at = None
    ENGS = ("pe", "act", "dve", "pool", "sp")
    NSLOT = 8

    def __init__(self, nc):
        self.nc = nc
        self.es = ExitStack()
        self.q = {e: [] for e in self.ENGS}
        self.cnt = {e: 0 for e in self.ENGS}
        self.seen = {e: {} for e in self.ENGS}
        self.sems = {}
        for e in ("pe", "act", "dve", "pool"):
            self.sems["c_" + e] = self.es.enter_context(nc.semaphore("c_" + e))
        self.dma_i = {}
        self.dma_tgt = {}
        for e in ("sp", "pool", "act"):
            self.dma_i[e] = 0
            for s in range(self.NSLOT):
                k = "d_%s_%d" % (e, s)
                self.sems[k] = self.es.enter_context(nc.semaphore(k))
                self.dma_tgt[k] = 0
        self.out_toks = []
        self.n_ins = 0
        self.n_t = 0
        self.ps = None

    skip_set = ()
    skip = False

    def phase_begin(self):
        self.ps = ExitStack()
        self.skip = getattr(self, "n_phase", 0) in self.skip_set

    def tile(self, name, shape, dt, psum=False):
        st = self.ps if self.ps is not None else self.es
        self.n_t += 1
        name = "%s_t%d" % (name, self.n_t)
        if psum:
            return st.enter_context(self.nc.psum_tensor(name, list(shape), dt))
        return st.enter_context(self.nc.sbuf_tensor(name, list(shape), dt))

    def barrier(self):
        for e in self.ENGS:
            seen = self.seen[e]
            for f in ("pe", "act", "dve", "pool"):
                k = "c_" + f
                if f != e and self.cnt[f] > 0 and seen.get(k, 0) < self.cnt[f]:
                    self.q[e].append(("wait", k, self.cnt[f]))
                    seen[k] = self.cnt[f]
            for k, v in self.dma_tgt.items():
                if v > 0 and seen.get(k, 0) < v:
                    self.q[e].append(("wait", k, v))
                    seen[k] = v

    def phase_end(self):
        self.skip = False
        self.barrier()
        self.flush()
        self.ps.close()
        self.ps = None
        self.n_phase = getattr(self, "n_phase", 0) + 1
        if self.stop_at is not None and self.n_phase >= self.stop_at:
            self.finish()
            raise StopBuild(self)

    def flush(self):
        nc = self.nc
        handles = {"pe": "tensor", "act": "scalar", "dve": "vector", "pool": "gpsimd", "sp": "sync"}
        with nc.Block() as block:
            for e in self.ENGS:
                q = self.q[e]
                sems = self.sems

                def body(eng, q=q, sems=sems):
                    for it in q:
                        if it[0] == "wait":
                            eng.wait_ge(sems[it[1]], it[2])
                        else:
                            ins = it[1](eng)
                            if it[2] is not None:
                                ins.then_inc(sems[it[2]], it[3])
                getattr(block, handles[e])(body)
        tot = getattr(self, "tot", {e: 0 for e in self.ENGS})
        for e in self.ENGS:
            tot[e] += len(self.q[e])
        self.tot = tot
        self.q = {e: [] for e in self.ENGS}

    def _deps(self, eng, reads, writes):
        deps = []
        for b in reads:
            if b.writer is not None:
                deps.append(b.writer)
        for b in writes:
            deps.extend(b.readers.values())
            if b.writer is not None:
                deps.append(b.writer)
        seen = self.seen[eng]
        for (k, v, src) in deps:
            if src == eng and eng == "pe":
                continue
            if seen.get(k, 0) >= v:
                continue
            self.q[eng].append(("wait", k, v))
            seen[k] = v

    def _mark(self, tok, reads, writes):
        for b in reads:
            b.readers[tok[0]] = tok
        for b in writes:
            b.writer = tok
            b.readers = {}

    def op(self, eng, fn, reads=(), writes=(), inc=True):
        if self.skip:
            return None
        self._deps(eng, reads, writes)
        k = "c_" + eng
        if inc:
            self.cnt[eng] += 1
            tok = (k, self.cnt[eng], eng)
        else:
            tok = (k, self.cnt[eng] + 1, eng)
        self.q[eng].append(("ins", fn, k if inc else None, 1))
        self._mark(tok, reads, writes)
        self.n_ins += 1
        return tok

    def dma(self, qe, out, in_, reads=(), writes=(), is_output=False, **kw):
        if self.skip:
            return None
        i = self.dma_i[qe]
        self.dma_i[qe] = i + 1
        k = "d_%s_%d" % (qe, i % self.NSLOT)
        prev = self.dma_tgt[k]
        seen = self.seen[qe]
        if prev > 0 and seen.get(k, 0) < prev:
            self.q[qe].append(("wait", k, prev))
            seen[k] = prev
        self._deps(qe, reads, writes)
        tgt = prev + 16
        self.dma_tgt[k] = tgt
        tok = (k, tgt, "dma")

        def fn(e, out=out, in_=in_, kw=kw):
            return e.dma_start(out=out, in_=in_, **kw)
        self.q[qe].append(("ins", fn, k, 16))
        self._mark(tok, reads, writes)
        if is_output:
            self.out_toks.append(tok)
        self.n_ins += 1
        return tok

    def finish(self):
        for (k, v, _) in self.out_toks:
            if self.seen["sp"].get(k, 0) < v:
                self.q["sp"].append(("wait", k, v))
                self.seen["sp"][k] = v
        self.flush()
        self.es.close()

    def mm(self, out, lhsT, rhs, start, stop, reads, writes, inc=None):
        self.op("pe", lambda e: e.matmul(out, lhsT=lhsT, rhs=rhs, start=start, stop=stop), reads, writes,
                inc=stop if inc is None else inc)

    def tr(self, out, in_, ident, reads, writes, inc=True):
        self.op("pe", lambda e: e.transpose(out=out, in_=in_, identity=ident), reads, writes, inc=inc)

    def act(self, out, in_, func, reads, writes, **kw):
        self.op("act", lambda e: e.activation(out=out, in_=in_, func=func, **kw), reads, writes)

    def tt(self, out, a, b, op, reads, writes, eng="dve"):
        self.op(eng, lambda e: e.tensor_tensor(out=out, in0=a, in1=b, op=op), reads, writes)

    def ts(self, out, a, s1, s2, op0, op1, reads, writes, eng="dve"):
        self.op(eng, lambda e: e.tensor_scalar(out=out, in0=a, scalar1=s1, scalar2=s2, op0=op0, op1=op1), reads, writes)

    def stt(self, out, a, s, b, op0, op1, reads, writes):
        self.op("dve", lambda e: e.scalar_tensor_tensor(out=out, in0=a, scalar=s, in1=b, op0=op0, op1=op1), reads, writes)

    def gelu(self, out, x, tm, rx, btm, wout, eng2="pool"):
        self.act(tm, x, AF.Square, rx, [btm])
        self.ts(tm, tm, 0.044715, 1.0, ALU.mult, ALU.add, [btm], [btm], eng=eng2)
        self.tt(tm, tm, x, ALU.mult, [btm] + rx, [btm])
        self.act(tm, tm, AF.Sigmoid, [btm], [btm], scale=1.5957691216057308)
        if out is not None:
            self.tt(out, tm, x, ALU.mult, [btm] + rx, wout)

    def cp(self, out, in_, reads, writes, eng="dve"):
        if eng == "act":
            self.act(out, in_, AF.Copy, reads, writes)
        else:
            self.op(eng, lambda e: e.tensor_copy(out=out, in_=in_), reads, writes)


def build(L, dbg=False, stop=None):
    nc = bass.Bass("TRN2", target_bir_lowering=False)
    Prog.stop_at = stop
    import os
    Prog.skip_set = tuple(int(v) for v in os.environ.get("KSKIP", "").split(",") if v)
    try:
        return _build(nc, L, dbg)
    except StopBuild as e:
        P = e.args[0] if e.args else None
        return nc, P


def _build(nc, L, dbg):

    def din(name, shape):
        return nc.dram_tensor(name, list(shape), F32, kind="ExternalInput").ap()

    x_in = din("x", [T, D])
    cst_in = din("cst", [128, NCS])
    colp_in = din("colp", [L, 128, NCP])
    ada_w = din("ada_w", [L, D, 6 * D])
    w_in = din("w_in", [L, D, INW])
    w1k = din("cmp_w1_k", [L, 32, 128, 128])
    w2k = din("cmp_w2_k", [L, 128, 128])
    w1v = din("cmp_w1_v", [L, 32, 128, 128])
    w2v = din("cmp_w2_v", [L, 128, 128])
    lru_wa = din("lru_wa", [L, 16, 128, 128])
    lru_wi = din("lru_wi", [L, 16, 128, 128])
    proj_a = din("proj_a", [L, D, D])
    proj_b = din("proj_b", [L, D, D])
    w_out = din("w_out", [L, D, D])
    ffn_up = din("ffn_up", [L, D, 2 * DFF])
    ffn_down = din("ffn_down", [L, DFF, D])
    out = nc.dram_tensor("out", [T, D], F32, kind="ExternalOutput").ap()
    sk = "ExternalOutput" if dbg else "Internal"

    def dsc(name, shape, dt):
        return nc.dram_tensor(name, list(shape), dt, kind=sk).ap()

    xT = dsc("s_xT", [D, T], F32)
    qT = dsc("s_qT", [D, T], BF16)
    kcT = dsc("s_kcT", [512, T], BF16)
    vcT = dsc("s_vcT", [512, T], BF16)
    ksT = dsc("s_ksT", [512, T], BF16)
    kwT = dsc("s_kwT", [512, T], BF16)
    vsd = dsc("s_vs", [T, 512], BF16)
    vwd = dsc("s_vw", [T, 512], BF16)
    gtd = dsc("s_gates", [T, 48], F32)
    lxT = dsc("s_lxT", [D, T], F32)
    glyT = dsc("s_glyT", [D, T], BF16)
    sgaT = dsc("s_sgaT", [D, T], BF16)
    sgbT = dsc("s_sgbT", [D, T], BF16)
    obT = dsc("s_obT", [D, T], BF16)
    oaT = dsc("s_oaT", [D, T], BF16)
    hidT = dsc("s_hidT", [DFF, T], BF16)

    P = Prog(nc)
    cst = P.tile("cst", [128, NCS], F32)
    colp = P.tile("colp", [128, L, NCP], F32)
    modc = P.tile("modc", [128, L, 96], F32)
    A1 = P.tile("A1", [128, L, 16], F32)
    A2 = P.tile("A2", [128, L, 16], F32)
    clam = P.tile("clam", [128, L, 16], F32)
    clam2 = P.tile("clam2", [128, L, 16], F32)
    cact = P.tile("cact", [128, 16], F32)
    identb = P.tile("identb", [128, 128], BF16)
    ovlb = P.tile("ovlb", [128, 32], BF16)
    onesf = P.tile("onesf", [128, 128], F32)
    arena = P.tile("arena", [128, 49152], BF16)
    hT = arena[:, 0:32768].rearrange("p (k t) -> p k t", t=T)
    arena_f = arena[:].bitcast(F32)

    def AR(o, n):
        return arena_f[:, o:o + n]

    def ARB(o, n):
        return arena[:, o:o + n]
    bH = [Buf() for _ in range(KC)]

    def C(name, a=0, b=None):
        o, w = CS[name]
        return cst[:, o + a:o + (w if b is None else b)]

    def CPc(l, name, a=0, b=None):
        o, w = CP[name]
        return colp[:, l, o + a:o + (w if b is None else b)]

    ident_f = C("ident")

    P.phase_begin()
    bC = Buf()
    P.dma("sp", cst[:], cst_in, writes=[bC])
    P.dma("sp", colp[:], colp_in.rearrange("l p n -> p l n"), writes=[bC])
    P.cp(identb[:], C("ident"), [bC], [bC])
    P.cp(ovlb[:], C("ovl"), [bC], [bC])
    P.op("dve", lambda e: e.memset(onesf[:], 1.0), [], [bC])
    e1 = P.tile("e1", [128, 16], F32)
    P.act(e1[:], C("cT"), AF.Exp, [bC], [bC], scale=-1.0)
    P.ts(e1[:], e1[:], 1.0, None, ALU.add, ALU.bypass, [bC], [bC])
    P.op("dve", lambda e: e.reciprocal(out=e1[:], in_=e1[:]), [bC], [bC])
    P.tt(cact[:], e1[:], C("cT"), ALU.mult, [bC], [bC])
    for l in range(L):
        P.act(clam[:, l, :], CPc(l, "llam"), AF.Exp, [bC], [bC], scale=-1.0)
    for l in range(L):
        P.act(clam[:, l, :], clam[:, l, :], AF.Ln, [bC], [bC], bias=1.0)
    for l in range(L):
        P.ts(clam2[:, l, :], clam[:, l, :], -16.0, None, ALU.mult, ALU.bypass, [bC], [bC])
        P.ts(clam[:, l, :], clam[:, l, :], -8.0, None, ALU.mult, ALU.bypass, [bC], [bC])
    NST = 2
    ast = [P.tile("ast%d" % i, [128, 16, 512], F32) for i in range(NST)]
    bast = [Buf() for _ in range(NST)]
    psm = P.tile("psm", [128, 96], F32, psum=True)
    bpsm = Buf()
    ci = 0
    for l in range(L):
        awv = ada_w[l].rearrange("(k p) m -> p k m", p=128)
        for ch in range(24):
            s = ci % NST
            ci += 1
            P.dma("sp", ast[s][:], awv[:, :, ch * 512:(ch + 1) * 512], writes=[bast[s]])
            for j in range(4):
                col = ch * 4 + j
                for k in range(KC):
                    P.mm(psm[:, col:col + 1], ast[s][:, k, j * 128:(j + 1) * 128], cact[:, k:k + 1],
                         k == 0, k == KC - 1, [bast[s], bC], [bpsm])
        P.tt(modc[:, l, :], psm[:], CPc(l, "adab"), ALU.add, [bpsm, bC], [bC, bpsm])
        P.ts(A1[:, l, :], modc[:, l, 16:32], 1.0, None, ALU.add, ALU.bypass, [bC], [bC])
        P.tt(A1[:, l, :], A1[:, l, :], CPc(l, "n1g"), ALU.mult, [bC], [bC])
        P.ts(A2[:, l, :], modc[:, l, 64:80], 1.0, None, ALU.add, ALU.bypass, [bC], [bC])
        P.tt(A2[:, l, :], A2[:, l, :], CPc(l, "n2g"), ALU.mult, [bC], [bC])
    P.phase_end()

    P.phase_begin()
    xt = [P.tile("xt%d" % i, [128, D], F32) for i in range(2)]
    bxt = [Buf() for _ in range(2)]
    xo = [P.tile("xo%d" % i, [128, 16, 128], F32) for i in range(2)]
    bxo = [Buf() for _ in range(2)]
    pst = [P.tile("pst%d" % i, [128, 4, 128], F32, psum=True) for i in range(4)]
    bpst = [Buf() for _ in range(4)]
    pi = 0
    xTv = xT.rearrange("(k p) t -> p k t", p=128)
    for tt_ in range(16):
        s = tt_ % 2
        P.dma("sp", xt[s][:], x_in[tt_ * 128:(tt_ + 1) * 128, :], writes=[bxt[s]])
        for g4 in range(4):
            pp = pi % 4
            pi += 1
            for j in range(4):
                fc = g4 * 4 + j
                P.tr(pst[pp][:, j, :], xt[s][:, fc * 128:(fc + 1) * 128], ident_f, [bxt[s]], [bpst[pp]], inc=(j == 3))
            P.cp(xo[s][:, g4 * 4:(g4 + 1) * 4, :], pst[pp][:], [bpst[pp]], [bxo[s]], eng=("act" if g4 % 2 else "dve"))
        P.dma("sp", xTv[:, :, tt_ * 128:(tt_ + 1) * 128], xo[s][:], reads=[bxo[s]])
    P.phase_end()

    def norm_phase(Acol, Bcol, final=False):
        P.phase_begin()
        NW = 256
        xc = [P.tile("xc%d" % i, [128, 16, NW], F32) for i in range(2)]
        bxc = [Buf() for _ in range(2)]
        sq = [P.tile("sq%d" % i, [128, NW], F32) for i in range(3)]
        bsq = [Buf() for _ in range(3)]
        pss = P.tile("pss", [128, 512], F32, psum=True)
        bpss = Buf()
        rinv = P.tile("rinv", [128, NW], F32)
        brinv = Buf()
        t1 = [P.tile("t1%d" % i, [128, NW], F32) for i in range(3)]
        bt1 = [Buf() for _ in range(3)]
        if final:
            ot = [P.tile("ot%d" % i, [128, D], F32) for i in range(4)]
            bot = [Buf() for _ in range(4)]
            psf = [P.tile("psf%d" % i, [128, 4, 128], F32, psum=True) for i in range(2)]
            bpsf = [Buf() for _ in range(2)]
        for n in range(T // NW):
            s = n % 2
            P.dma("sp", xc[s][:], xTv[:, :, n * NW:(n + 1) * NW], writes=[bxc[s]])
            for fc in range(KC):
                q3 = fc % 3
                P.act(sq[q3][:], xc[s][:, fc, :], AF.Square, [bxc[s]], [bsq[q3]])
                P.mm(pss[:, 0:NW], onesf[:], sq[q3][:], fc == 0, fc == KC - 1, [bsq[q3]], [bpss], inc=True)
            P.act(rinv[:], pss[:, 0:NW], AF.Sqrt, [bpss], [brinv], bias=1e-6, scale=1.0 / D)
            P.op("dve", lambda e: e.reciprocal(out=rinv[:], in_=rinv[:]), [brinv], [brinv])
            for fc in range(KC):
                q3 = fc % 3
                P.tt(t1[q3][:], xc[s][:, fc, :], rinv[:], ALU.mult, [bxc[s], brinv], [bt1[q3]])
                if not final:
                    P.act(hT[:, fc, n * NW:(n + 1) * NW], t1[q3][:], AF.Identity, [bt1[q3]], [bH[fc]],
                          scale=Acol[:, fc:fc + 1], bias=Bcol[:, fc:fc + 1])
                else:
                    P.act(t1[q3][:], t1[q3][:], AF.Identity, [bt1[q3]], [bt1[q3]], scale=Acol[:, fc:fc + 1])
                    pf = fc % 2
                    for j in range(2):
                        P.tr(psf[pf][:, j, :], t1[q3][:, j * 128:(j + 1) * 128], ident_f, [bt1[q3]], [bpsf[pf]], inc=(j == 1))
                    for j in range(2):
                        oj = (n % 2) * 2 + j
                        P.cp(ot[oj][:, fc * 128:(fc + 1) * 128], psf[pf][:, j, :], [bpsf[pf]], [bot[oj]],
                             eng=("act" if pf else "dve"))
            if final:
                for j in range(2):
                    oj = (n % 2) * 2 + j
                    r0 = n * NW + j * 128
                    P.dma("sp", out[r0:r0 + 128, :], ot[oj][:], reads=[bot[oj]], is_output=True)
        P.phase_end()

    class WStream:
        def __init__(self, nb=3, width=8192):
            self.t = [P.tile("wb%d" % i, [128, width], BF16) for i in range(nb)]
            self.b = [Buf() for _ in range(nb)]
            self.i = 0
            self.nb = nb

        def load(self, wsrc, kc, ncols):
            s = self.i % self.nb
            self.i += 1
            v = self.t[s][:, 0:kc * ncols].rearrange("p (k m) -> p k m", m=ncols)
            P.dma("pool", v, wsrc.rearrange("(k p) m -> p k m", p=128), writes=[self.b[s]])
            return v, self.b[s]

    for l in range(L):
        sh1, g1c = modc[:, l, 0:16], modc[:, l, 32:48]
        sh2, g2c = modc[:, l, 48:64], modc[:, l, 80:96]
        norm_phase(A1[:, l, :], sh1)

        P.phase_begin()
        ws = WStream()
        psz = [P.tile("psz%d" % i, [128, T], F32, psum=True) for i in range(2)]
        bpsz = [Buf() for _ in range(2)]
        ob16 = [P.tile("ob16_%d" % i, [128, T], BF16) for i in range(3)]
        bob16 = [Buf() for _ in range(3)]
        of32 = [P.tile("of32_%d" % i, [128, T], F32) for i in range(2)]
        bof32 = [Buf() for _ in range(2)]
        cnt = {"p": 0, "o": 0, "f": 0}

        def fm_chunk(col0, kind, dst, row0):
            wv, bw = ws.load(w_in[l][:, col0:col0 + 512], KC, 512)
            for m in range(4):
                pp = cnt["p"] % 2
                cnt["p"] += 1
                for k in range(KC):
                    for n in range(4):
                        P.mm(psz[pp][:, n * 512:(n + 1) * 512], wv[:, k, m * 128:(m + 1) * 128],
                             hT[:, k, n * 512:(n + 1) * 512], k == 0, k == KC - 1, [bw, bH[k]], [bpsz[pp]],
                             inc=(k == KC - 1 and n == 3))
                r = row0 + m * 128
                if kind == "lx":
                    o = cnt["f"] % 2
                    cnt["f"] += 1
                    P.cp(of32[o][:], psz[pp][:], [bpsz[pp]], [bof32[o]], eng="dve")
                    P.dma("sp", dst[r:r + 128, :], of32[o][:], reads=[bof32[o]])
                else:
                    o = cnt["o"] % 3
                    cnt["o"] += 1
                    if kind == "q":
                        P.act(ob16[o][:], psz[pp][:], AF.Copy, [bpsz[pp]], [bob16[o]], scale=float(128.0 ** -0.5))
                    elif kind == "copy":
                        P.cp(ob16[o][:], psz[pp][:], [bpsz[pp]], [bob16[o]], eng="dve")
                    elif kind == "gelu":
                        o2 = cnt["f"] % 2
                        cnt["f"] += 1
                        P.gelu(ob16[o][:], psz[pp][:], of32[o2][:], [bpsz[pp]], bof32[o2], [bob16[o]])
                    elif kind == "sig":
                        P.act(ob16[o][:], psz[pp][:], AF.Sigmoid, [bpsz[pp]], [bob16[o]])
                    P.dma("sp", dst[r:r + 128, :], ob16[o][:], reads=[bob16[o]])

        def tm_chunk(col0, ncols, dst, f32out):
            wv, bw = ws.load(w_in[l][:, col0:col0 + ncols], KC, ncols)
            for tt_ in range(16):
                pp = cnt["p"] % 2
                cnt["p"] += 1
                for k in range(KC):
                    P.mm(psz[pp][:, 0:ncols], hT[:, k, tt_ * 128:(tt_ + 1) * 128], wv[:, k, :],
                         k == 0, k == KC - 1, [bw, bH[k]], [bpsz[pp]])
                if f32out:
                    o = cnt["f"] % 2
                    cnt["f"] += 1
                    P.cp(of32[o][:, 0:ncols], psz[pp][:, 0:ncols], [bpsz[pp]], [bof32[o]], eng="dve")
                    P.dma("sp", dst[tt_ * 128:(tt_ + 1) * 128, :], of32[o][:, 0:ncols], reads=[bof32[o]])
                else:
                    o = cnt["o"] % 3
                    cnt["o"] += 1
                    P.cp(ob16[o][:, 0:ncols], psz[pp][:, 0:ncols], [bpsz[pp]], [bob16[o]], eng="act")
                    P.dma("sp", dst[tt_ * 128:(tt_ + 1) * 128, :], ob16[o][:, 0:ncols], reads=[bob16[o]])

        for i in range(4):
            fm_chunk(i * 512, "q", qT, i * 512)
        fm_chunk(2048, "copy", kcT, 0)
        fm_chunk(2560, "copy", vcT, 0)
        fm_chunk(3072, "copy", ksT, 0)
        tm_chunk(3584, 512, vsd, False)
        fm_chunk(4096, "copy", kwT, 0)
        tm_chunk(4608, 512, vwd, False)
        tm_chunk(5120, 48, gtd, True)
        for i in range(4):
            fm_chunk(5168 + i * 512, "lx", lxT, i * 512)
        for i in range(4):
            fm_chunk(7216 + i * 512, "gelu", glyT, i * 512)
        for i in range(4):
            fm_chunk(9264 + i * 512, "sig", sgaT, i * 512)
        for i in range(4):
            fm_chunk(11312 + i * 512, "sig", sgbT, i * 512)
        P.phase_end()

        P.phase_begin()
        wab = P.tile("wab", [128, 16, 128], BF16)
        wib = P.tile("wib", [128, 16, 128], BF16)
        bwg = Buf()
        P.dma("pool", wab[:], lru_wa[l].rearrange("n d e -> d n e"), writes=[bwg])
        P.dma("pool", wib[:], lru_wi[l].rearrange("n d e -> d n e"), writes=[bwg])
        xp = [AR(10240 + i * 2064, T + 3) for i in range(2)]
        bxp = [Buf() for _ in range(2)]
        gly = [P.tile("gly%d" % i, [128, T], BF16) for i in range(2)]
        bgly = [Buf() for _ in range(2)]
        for i in range(2):
            P.op("pool", lambda e, i=i: e.memset(xp[i][:, 0:3], 0.0), [], [bxp[i]])
        u = AR(0, T); bu = Buf()
        ub = P.tile("ub", [128, T], BF16); bub = Buf()
        rr = AR(2048, T); brr = Buf()
        ii = AR(4096, T); bii = Buf()
        aa = AR(6144, T); baa = Buf()
        mmx = AR(8192, T); bmm = Buf()
        hh = rr; bhh = brr
        obt = [P.tile("obt%d" % i, [128, T], BF16) for i in range(2)]
        bobt = [Buf() for _ in range(2)]
        psr = P.tile("psr", [128, T], F32, psum=True); bpsr = Buf()
        psi = P.tile("psi", [128, T], F32, psum=True); bpsi = Buf()
        for ct in range(16):
            s = ct % 2
            P.dma("sp", xp[s][:, 3:T + 3], lxT[ct * 128:(ct + 1) * 128, :], writes=[bxp[s]])
            P.dma("sp", gly[s][:], glyT[ct * 128:(ct + 1) * 128, :], writes=[bgly[s]])
            lcw = CPc(l, "lcw", ct * 4, ct * 4 + 4)
            P.ts(u[:], xp[s][:, 3:T + 3], lcw[:, 3:4], CPc(l, "lcb", ct, ct + 1), ALU.mult, ALU.add, [bxp[s]], [bu])
            for j in (2, 1, 0):
                P.stt(u[:], xp[s][:, j:j + T], lcw[:, j:j + 1], u[:], ALU.mult, ALU.add, [bxp[s], bu], [bu])
            P.cp(ub[:], u[:], [bu], [bub], eng="pool")
            for n in range(4):
                P.mm(psr[:, n * 512:(n + 1) * 512], wab[:, ct, :], ub[:, n * 512:(n + 1) * 512], True, True,
                     [bwg, bub], [bpsr], inc=(n == 3))
            for n in range(4):
                P.mm(psi[:, n * 512:(n + 1) * 512], wib[:, ct, :], ub[:, n * 512:(n + 1) * 512], True, True,
                     [bwg, bub], [bpsi], inc=(n == 3))
            P.act(rr[:], psr[:], AF.Sigmoid, [bpsr], [brr], bias=CPc(l, "lba", ct, ct + 1))
            P.act(ii[:], psi[:], AF.Sigmoid, [bpsi], [bii], bias=CPc(l, "lbi", ct, ct + 1))
            P.act(aa[:], rr[:], AF.Exp, [brr], [baa], scale=clam[:, l, ct:ct + 1])
            P.act(mmx[:], rr[:], AF.Exp, [brr], [bmm], scale=clam2[:, l, ct:ct + 1])
            P.ts(mmx[:], mmx[:], -1.0, 1.0, ALU.mult, ALU.add, [bmm], [bmm], eng="pool")
            P.ts(mmx[:], mmx[:], 1e-20, None, ALU.max, ALU.bypass, [bmm], [bmm], eng="pool")
            P.act(mmx[:], mmx[:], AF.Sqrt, [bmm], [bmm])
            P.tt(ii[:], ii[:], u[:], ALU.mult, [bii, bu], [bii])
            P.tt(ii[:], ii[:], mmx[:], ALU.mult, [bii, bmm], [bii])
            P.op("dve", lambda e: e.tensor_tensor_scan(out=hh[:], data0=aa[:], data1=ii[:], initial=0.0,
                                                       op0=ALU.mult, op1=ALU.add), [baa, bii], [bhh])
            P.tt(obt[s][:], hh[:], gly[s][:], ALU.mult, [bhh, bgly[s]], [bobt[s]])
            P.dma("sp", obT[ct * 128:(ct + 1) * 128, :], obt[s][:], reads=[bobt[s]])
        P.phase_end()

        P.phase_begin()
        gts = P.tile("gts", [128, 16, 48], F32); bg = Buf()
        P.dma("sp", gts[:], gtd.rearrange("(tt p) c -> p tt c", p=128), writes=[bg])
        P.act(gts[:], gts[:], AF.Sigmoid, [bg], [bg])
        w1b = [P.tile("w1b%d" % i, [128, 32, 128], BF16) for i in range(2)]
        w2b = [P.tile("w2b%d" % i, [128, 128], BF16) for i in range(2)]
        posb = [P.tile("posb%d" % i, [128, 32], BF16) for i in range(2)]
        bw1 = Buf()
        for i, (w1, w2, pn) in enumerate(((w1k, w2k, "posk"), (w1v, w2v, "posv"))):
            P.dma("pool", w1b[i][:], w1[l].rearrange("s d e -> d s e"), writes=[bw1])
            P.dma("pool", w2b[i][:], w2[l], writes=[bw1])
            P.cp(posb[i][:], CPc(l, pn), [], [bw1])
        ckk = P.tile("ckk", [128, 2], F32); bck = Buf()
        kct = ARB(24576, T); vct = ARB(26624, T); bkv = Buf()
        kst = ARB(28672, T); kwt = ARB(30720, T); bks = Buf()
        vst = P.tile("vst", [128, 16, 128], BF16); vwt = P.tile("vwt", [128, 16, 128], BF16); bvs = Buf()
        qt = [ARB(i * T, T) for i in range(4)]
        bq = [Buf() for _ in range(4)]
        hid = P.tile("hid", [128, 128], BF16); bhid = Buf()
        xh = P.tile("xh", [128, 128], F32); bxh = Buf()
        th_ = P.tile("th_", [128, 128], F32); bth = Buf()
        kcmp = P.tile("kcmp", [128, 128], BF16); vcmp = P.tile("vcmp", [128, 128], BF16); bcmp = Buf()
        oast = [ARB(8192 + i * T, T) for i in range(4)]
        boast = [Buf() for _ in range(4)]
        oacc = [P.tile("oacc%d" % i, [128, 128], F32) for i in range(4)]
        boacc = [Buf() for _ in range(4)]
        sm = AR(8192, T); bsm = Buf()
        pb = ARB(32768, T); bpb = Buf()
        pT = ARB(34816, T).rearrange("p (k q) -> p k q", q=128); bpT = Buf()
        biasv = AR(10240, T); bbias = Buf()
        biasf = biasv.rearrange("p (a b) -> p a b", b=64)
        sml = P.tile("sml", [128, 64], F32); bsml = Buf()
        sc = P.tile("sc", [128, 32], F32); sc2 = P.tile("sc2", [128, 32], F32); bsc = Buf()
        m8 = P.tile("m8", [128, 16], F32)
        selb = P.tile("selb", [128, 32], F32)
        obf = P.tile("obf", [128, 128], BF16); bobf = Buf()
        ps_impt = P.tile("ps_imp", [128, 512], F32, psum=True); bpimp = Buf()
        ps_imp = ps_impt[:, 0:32]
        ps_m = P.tile("ps_m", [128, 512], F32, psum=True); bpm = Buf()
        ps_s = P.tile("ps_s", [128, T], F32, psum=True); bps = Buf()
        ps_t = P.tile("ps_t", [128, 16, 128], BF16, psum=True); bpt = Buf()
        P.op("dve", lambda e: e.memset(hid[:], 0.0), [], [bhid])
        P.op("dve", lambda e: e.memset(kcmp[:], 0.0), [], [bcmp])
        P.op("dve", lambda e: e.memset(vcmp[:], 0.0), [], [bcmp])

        def softmax_pv(h, i, nk, bias_ap, bias_bufs, kT, k0, vT, vb0, gcol, last):
            qs = qt[h][:, i * 128:(i + 1) * 128]
            nch = (nk + 511) // 512
            for c in range(nch):
                w = min(512, nk - c * 512)
                P.mm(ps_s[:, c * 512:c * 512 + w], qs, kT[:, k0 + c * 512:k0 + c * 512 + w], True, True,
                     [bq[h], bks], [bps], inc=(c == nch - 1))
            P.tt(sm[:, 0:nk], ps_s[:, 0:nk], bias_ap, ALU.add, [bps] + bias_bufs, [bsm])
            P.op("dve", lambda e: e.reduce_max(out=sml[:, 0:1], in_=sm[:, 0:nk], axis=AX.X, negate=True), [bsm], [bsml])
            P.act(pb[:, 0:nk], sm[:, 0:nk], AF.Exp, [bsm, bsml], [bpb, bsml], bias=sml[:, 0:1], accum_out=sml[:, 1:2])
            nb = nk // 128
            for kb in range(nb):
                P.tr(ps_t[:, kb, :], pb[:, kb * 128:(kb + 1) * 128], identb[:], [bpb], [bpt], inc=(kb == nb - 1))
            P.cp(pT[:, 0:nb, :], ps_t[:, 0:nb, :], [bpt], [bpT], eng="act")
            for kb in range(nb):
                P.mm(ps_m[:, 0:128], pT[:, kb, :], vT[:, vb0 + kb, :], kb == 0, kb == nb - 1, [bpT, bvs], [bpm])
            P.op("dve", lambda e: e.reciprocal(out=sml[:, 2:3], in_=sml[:, 1:2]), [bsml], [bsml])
            P.tt(sml[:, 3:4], sml[:, 2:3], gts[:, i, gcol:gcol + 1], ALU.mult, [bsml, bg], [bsml])
            dst = obf[:] if last else oacc[h][:]
            P.stt(dst, ps_m[:, 0:128], sml[:, 3:4], oacc[h][:], ALU.mult, ALU.add, [bpm, bsml, boacc[h]],
                  [bobf] if last else [boacc[h]])

        for g in range(4):
            P.dma("sp", kct[:], kcT[g * 128:(g + 1) * 128, :], writes=[bkv])
            P.dma("sp", vct[:], vcT[g * 128:(g + 1) * 128, :], writes=[bkv])
            P.dma("sp", kst[:], ksT[g * 128:(g + 1) * 128, :], writes=[bks])
            P.dma("sp", kwt[:], kwT[g * 128:(g + 1) * 128, :], writes=[bks])
            P.dma("sp", vst[:], vsd[:, g * 128:(g + 1) * 128].rearrange("(tt p) d -> p tt d", p=128), writes=[bvs])
            P.dma("sp", vwt[:], vwd[:, g * 128:(g + 1) * 128].rearrange("(tt p) d -> p tt d", p=128), writes=[bvs])
            for h in range(4):
                hd = g * 4 + h
                P.dma("sp", qt[h][:], qT[hd * 128:(hd + 1) * 128, :], writes=[bq[h]])
            for i, src in enumerate((kct, vct)):
                for s_ in range(32):
                    P.mm(ps_m[:, 256 + i:257 + i], w1b[i][:, s_, :], posb[i][:, s_:s_ + 1], s_ == 0, s_ == 31, [bw1], [bpm])
                P.cp(ckk[:, i:i + 1], ps_m[:, 256 + i:257 + i], [bpm], [bck])
                for s_ in range(32):
                    P.mm(ps_m[:, 0:127], w1b[i][:, s_, :], src[:, s_:s_ + 2017:16], s_ == 0, s_ == 31, [bw1, bkv], [bpm])
                P.act(xh[:, 0:127], ps_m[:, 0:127], AF.Identity, [bpm, bck], [bxh], bias=ckk[:, i:i + 1])
                P.gelu(hid[:, 0:127], xh[:, 0:127], th_[:, 0:127], [bxh], bth, [bhid])
                if i == 0:
                    P.mm(ps_m[:, 128:255], w2b[0][:], hid[:, 0:127], True, True, [bw1, bhid], [bpm])
                    P.cp(kcmp[:, 0:127], ps_m[:, 128:255], [bpm], [bcmp])
                else:
                    P.mm(ps_m[0:127, 128:256], hid[:, 0:127], w2b[1][:], True, True, [bw1, bhid], [bpm])
                    P.cp(vcmp[0:127, :], ps_m[0:127, 128:256], [bpm], [bcmp])
            for i in range(16):
                for h in range(4):
                    hd = g * 4 + h
                    P.mm(ps_m[:, 0:128], qt[h][:, i * 128:(i + 1) * 128], kcmp[:], True, True, [bq[h], bcmp], [bpm])
                    P.tt(sm[:, 0:128], ps_m[:, 0:128], C("cmpb", i * 128, (i + 1) * 128), ALU.add, [bpm], [bsm])
                    P.op("dve", lambda e: e.reduce_max(out=sml[:, 0:1], in_=sm[:, 0:128], axis=AX.X, negate=True), [bsm], [bsml])
                    P.act(sm[:, 128:256], sm[:, 0:128], AF.Exp, [bsm, bsml], [bsm, bsml], bias=sml[:, 0:1], accum_out=sml[:, 1:2])
                    P.ts(sml[:, 1:2], sml[:, 1:2], 1e-30, None, ALU.max, ALU.bypass, [bsml], [bsml])
                    P.op("dve", lambda e: e.reciprocal(out=sml[:, 2:3], in_=sml[:, 1:2]), [bsml], [bsml])
                    P.tt(sml[:, 2:3], sml[:, 2:3], C("rowv", i, i + 1), ALU.mult, [bsml], [bsml])
                    P.ts(pb[:, 0:128], sm[:, 128:256], sml[:, 2:3], None, ALU.mult, ALU.bypass, [bsm, bsml], [bpb])
                    P.tr(ps_t[:, 0, :], pb[:, 0:128], identb[:], [bpb], [bpt])
                    P.cp(pT[:, 0, :], ps_t[:, 0, :], [bpt], [bpT], eng="act")
                    P.mm(ps_m[:, 128:256], pT[0:127, 0, :], vcmp[0:127, :], True, True, [bpT, bcmp], [bpm])
                    P.mm(ps_imp, pT[0:127, 0, :], ovlb[0:127, :], h == 0, h == 3, [bpT], [bpimp])
                    P.ts(oacc[h][:], ps_m[:, 128:256], gts[:, i, hd * 3:hd * 3 + 1], None, ALU.mult, ALU.bypass,
                         [bpm, bg], [boacc[h]])
                P.tt(sc[:], ps_imp, C("validm", i * 32, (i + 1) * 32), ALU.mult, [bpimp], [bsc])
                P.tt(sc[:], sc[:], C("addc", i * 32, (i + 1) * 32), ALU.add, [bsc], [bsc])
                P.op("dve", lambda e: e.max(out=m8[:, 0:8], in_=sc[:]), [bsc], [bsc])
                P.op("dve", lambda e: e.match_replace(out=sc2[:], in_to_replace=m8[:, 0:8], in_values=sc[:], imm_value=-1e9), [bsc], [bsc])
                P.op("dve", lambda e: e.max(out=m8[:, 8:16], in_=sc2[:]), [bsc], [bsc])
                P.ts(selb[:], sc[:], m8[:, 15:16], NEG, ALU.is_lt, ALU.mult, [bsc], [bsc])
                nbk = 2 * (i + 1)
                P.cp(biasf[:, 0:nbk, :], selb[:, 0:nbk].unsqueeze(2).broadcast_to([128, nbk, 64]), [bsc], [bbias])
                P.tt(biasv[:, i * 128:(i + 1) * 128], biasv[:, i * 128:(i + 1) * 128], C("winb", 512, 640), ALU.add,
                     [bbias], [bbias])
                for h in range(4):
                    hd = g * 4 + h
                    softmax_pv(h, i, 128 * (i + 1), biasv[:, 0:128 * (i + 1)], [bbias], kst, 0, vst, 0, hd * 3 + 1, False)
                    j0 = max(0, i - 4)
                    nkw = (i - j0 + 1) * 128
                    softmax_pv(h, i, nkw, C("winb", 640 - nkw, 640), [], kwt, j0 * 128, vwt, j0, hd * 3 + 2, True)
                    P.tr(ps_t[:, 1, :], obf[:], identb[:], [bobf], [bpt])
                    P.cp(oast[h][:, i * 128:(i + 1) * 128], ps_t[:, 1, :], [bpt], [boast[h]], eng="act")
            for h in range(4):
                hd = g * 4 + h
                P.dma("sp", oaT[hd * 128:(hd + 1) * 128, :], oast[h][:], reads=[boast[h]])
        P.phase_end()

        P.phase_begin()
        ws = WStream()
        HT = 1024
        oah = arena[:, 0:16384].rearrange("p (k t) -> p k t", t=HT)
        obh = arena[:, 16384:32768].rearrange("p (k t) -> p k t", t=HT)
        mth = arena[:, 32768:49152].rearrange("p (k t) -> p k t", t=HT)
        boah, bobh, bmth = Buf(), Buf(), Buf()
        psy = [P.tile("psy%d" % i, [128, HT], F32, psum=True) for i in range(4)]
        bpsy = [Buf() for _ in range(4)]
        sgt = [P.tile("sgt%d" % i, [128, 2, HT], BF16) for i in range(2)]
        bsgt = [Buf() for _ in range(2)]
        m1 = [P.tile("m1_%d" % i, [128, HT], F32) for i in range(2)]
        bm1 = [Buf() for _ in range(2)]
        xo_ = [P.tile("xold%d" % i, [128, HT], F32) for i in range(2)]
        bxo_ = [Buf() for _ in range(2)]
        for th in range(2):
            t0 = th * HT
            for k4 in range(4):
                P.dma("sp", oah[:, k4 * 4:(k4 + 1) * 4, :],
                      oaT[k4 * 512:(k4 + 1) * 512, t0:t0 + HT].rearrange("(k p) t -> p k t", p=128), writes=[boah])
                P.dma("sp", obh[:, k4 * 4:(k4 + 1) * 4, :],
                      obT[k4 * 512:(k4 + 1) * 512, t0:t0 + HT].rearrange("(k p) t -> p k t", p=128), writes=[bobh])
            fi = 0
            for c4 in range(4):
                wa_, bwa = ws.load(proj_a[l][:, c4 * 512:(c4 + 1) * 512], KC, 512)
                wb_, bwb = ws.load(proj_b[l][:, c4 * 512:(c4 + 1) * 512], KC, 512)
                for m in range(4):
                    f = c4 * 4 + m
                    s = fi % 2
                    fi += 1
                    pa, pbb = 2 * s, 2 * s + 1
                    P.dma("sp", sgt[s][:, 0, :], sgaT[f * 128:(f + 1) * 128, t0:t0 + HT], writes=[bsgt[s]])
                    P.dma("sp", sgt[s][:, 1, :], sgbT[f * 128:(f + 1) * 128, t0:t0 + HT], writes=[bsgt[s]])
                    for (pp, wv, bw, src, bsrc) in ((pa, wa_, bwa, oah, boah), (pbb, wb_, bwb, obh, bobh)):
                        for k in range(KC):
                            for n in range(2):
                                P.mm(psy[pp][:, n * 512:(n + 1) * 512], wv[:, k, m * 128:(m + 1) * 128],
                                     src[:, k, n * 512:(n + 1) * 512], k == 0, k == KC - 1, [bw, bsrc], [bpsy[pp]],
                                     inc=(k == KC - 1 and n == 1))
                    P.tt(m1[s][:], psy[pa][:], sgt[s][:, 0, :], ALU.mult, [bpsy[pa], bsgt[s]], [bm1[s]])
                    P.tt(sgt[s][:, 1, :], psy[pbb][:], sgt[s][:, 1, :], ALU.mult, [bpsy[pbb], bsgt[s]], [bsgt[s]])
                    P.tt(mth[:, f, :], m1[s][:], sgt[s][:, 1, :], ALU.add, [bm1[s], bsgt[s]], [bmth], eng="pool")
            fi = 0
            for c4 in range(4):
                wo_, bwo = ws.load(w_out[l][:, c4 * 512:(c4 + 1) * 512], KC, 512)
                for m in range(4):
                    f = c4 * 4 + m
                    pp = fi % 4
                    s = fi % 2
                    fi += 1
                    P.dma("sp", xo_[s][:], xT[f * 128:(f + 1) * 128, t0:t0 + HT], writes=[bxo_[s]])
                    for k in range(KC):
                        for n in range(2):
                            P.mm(psy[pp][:, n * 512:(n + 1) * 512], wo_[:, k, m * 128:(m + 1) * 128],
                                 mth[:, k, n * 512:(n + 1) * 512], k == 0, k == KC - 1, [bwo, bmth], [bpsy[pp]],
                                 inc=(k == KC - 1 and n == 1))
                    P.stt(xo_[s][:], psy[pp][:], g1c[:, f:f + 1], xo_[s][:], ALU.mult, ALU.add, [bpsy[pp], bxo_[s]], [bxo_[s]])
                    P.dma("sp", xT[f * 128:(f + 1) * 128, t0:t0 + HT], xo_[s][:], reads=[bxo_[s]])
        P.phase_end()

        norm_phase(A2[:, l, :], sh2)

        P.phase_begin()
        ws = WStream()
        psg = [P.tile("psg%d" % i, [128, 1024], F32, psum=True) for i in range(4)]
        bpsg = [Buf() for _ in range(4)]
        gp = [P.tile("gp%d" % i, [128, T + 2], F32) for i in range(2)]
        bgp = [Buf() for _ in range(2)]
        vv = [AR(16384 + i * T, T) for i in range(2)]
        bvv = [Buf() for _ in range(2)]
        tmpg = AR(16384 + 2 * T, T); btmpg = Buf()
        gc = P.tile("gc", [128, T], F32); bgc = Buf()
        hb = [P.tile("hb%d" % i, [128, T], BF16) for i in range(2)]
        bhb = [Buf() for _ in range(2)]
        for i in range(2):
            P.op("pool", lambda e, i=i: e.memset(gp[i][:, 0:2], 0.0), [], [bgp[i]])
        for c4 in range(12):
            wg_, bwg_ = ws.load(ffn_up[l][:, c4 * 512:(c4 + 1) * 512], KC, 512)
            wv_, bwv_ = ws.load(ffn_up[l][:, DFF + c4 * 512:DFF + (c4 + 1) * 512], KC, 512)
            for m in range(4):
                j = c4 * 4 + m
                s = j % 2
                for hf in range(2):
                    for (pp, wv, bw) in ((2 * hf, wg_, bwg_), (2 * hf + 1, wv_, bwv_)):
                        for k in range(KC):
                            for n in range(2):
                                tn = hf * 1024 + n * 512
                                P.mm(psg[pp][:, n * 512:(n + 1) * 512], wv[:, k, m * 128:(m + 1) * 128],
                                     hT[:, k, tn:tn + 512], k == 0, k == KC - 1, [bw, bH[k]], [bpsg[pp]],
                                     inc=(k == KC - 1 and n == 1))
                    P.cp(gp[s][:, 2 + hf * 1024:2 + (hf + 1) * 1024], psg[2 * hf][:], [bpsg[2 * hf]], [bgp[s]], eng="act")
                    P.cp(vv[s][:, hf * 1024:(hf + 1) * 1024], psg[2 * hf + 1][:], [bpsg[2 * hf + 1]], [bvv[s]], eng="act")
                fw_ = CPc(l, "fcw", j * 3, j * 3 + 3)
                P.ts(gc[:], gp[s][:, 2:T + 2], fw_[:, 2:3], CPc(l, "fcb", j, j + 1), ALU.mult, ALU.add, [bgp[s]], [bgc])
                P.stt(gc[:], gp[s][:, 1:T + 1], fw_[:, 1:2], gc[:], ALU.mult, ALU.add, [bgp[s], bgc], [bgc])
                P.stt(gc[:], gp[s][:, 0:T], fw_[:, 0:1], gc[:], ALU.mult, ALU.add, [bgp[s], bgc], [bgc])
                P.gelu(None, gc[:], tmpg, [bgc], btmpg, None)
                P.tt(tmpg, tmpg, gc[:], ALU.mult, [btmpg, bgc], [btmpg])
                P.tt(hb[s][:], tmpg, vv[s], ALU.mult, [btmpg, bvv[s]], [bhb[s]], eng="pool")
                P.dma("sp", hidT[j * 128:(j + 1) * 128, :], hb[s][:], reads=[bhb[s]])
        P.phase_end()

        P.phase_begin()
        ws = WStream()
        hdh = arena[:, 0:49152].rearrange("p (k t) -> p k t", t=HT)
        bhd = [Buf() for _ in range(6)]
        psd = [P.tile("psd%d" % i, [128, HT], F32, psum=True) for i in range(4)]
        bpsd = [Buf() for _ in range(4)]
        xo2 = [P.tile("xold2_%d" % i, [128, HT], F32) for i in range(2)]
        bxo2 = [Buf() for _ in range(2)]
        for th in range(2):
            t0 = th * HT
            for k8 in range(6):
                P.dma("sp", hdh[:, k8 * 8:(k8 + 1) * 8, :],
                      hidT[k8 * 1024:(k8 + 1) * 1024, t0:t0 + HT].rearrange("(k p) t -> p k t", p=128), writes=[bhd[k8]])
            fi = 0
            for c2 in range(16):
                wd_, bwd = ws.load(ffn_down[l][:, c2 * 128:(c2 + 1) * 128], 48, 128)
                for m in range(1):
                    f = c2
                    pp = fi % 4
                    s = fi % 2
                    fi += 1
                    P.dma("sp", xo2[s][:], xT[f * 128:(f + 1) * 128, t0:t0 + HT], writes=[bxo2[s]])
                    for k in range(48):
                        for n in range(2):
                            P.mm(psd[pp][:, n * 512:(n + 1) * 512], wd_[:, k, m * 128:(m + 1) * 128],
                                 hdh[:, k, n * 512:(n + 1) * 512], k == 0, k == 47, [bwd, bhd[k // 8]], [bpsd[pp]],
                                 inc=(k == 47 and n == 1))
                    P.stt(xo2[s][:], psd[pp][:], g2c[:, f:f + 1], xo2[s][:], ALU.mult, ALU.add, [bpsd[pp], bxo2[s]], [bxo2[s]])
                    P.dma("sp", xT[f * 128:(f + 1) * 128, t0:t0 + HT], xo2[s][:], reads=[bxo2[s]])
        P.phase_end()

    fg = C("fing")
    norm_phase(fg, None, final=True)
    P.finish()
    return nc, P


def _consts():
    q = np.arange(128)
    cst = np.zeros((128, NCS), np.float32)

    def put(name, arr):
        o, w = CS[name]
        cst[:, o:o + w] = arr.reshape(128, w)
    t = (np.arange(16)[None, :] * 128 + q[:, None])
    put("rowv", (t >= 31).astype(np.float32))
    n = np.arange(128)
    cm = (16 * n[None, None, :] + 31 <= t[:, :, None]) & (n[None, None, :] < 127)
    put("cmpb", np.where(cm, 0.0, NEG).astype(np.float32))
    wb = np.zeros((128, 640), np.float32)
    kk = np.arange(128)
    wb[:, 0:128] = np.where(kk[None, :] > q[:, None], 0.0, NEG)
    wb[:, 512:640] = np.where(kk[None, :] <= q[:, None], 0.0, NEG)
    put("winb", wb)
    blk = np.arange(32)[None, None, :]
    cur = (t // 64)[:, :, None]
    valid = blk <= cur
    forced = (blk == 0) | (valid & (blk > cur - 2))
    put("validm", valid.astype(np.float32))
    put("addc", np.where(forced, 1e4, np.where(valid, 0.0, -1.0)).astype(np.float32))
    cs = np.arange(128) * 16
    ss = np.arange(32) * 64
    ov = ((cs[:, None] < ss[None, :] + 64) & (cs[:, None] + 32 > ss[None, :])).astype(np.float32)
    ov[127, :] = 0.0
    put("ovl", ov)
    put("ident", np.eye(128, dtype=np.float32))
    return cst


def _col(v, n):
    return np.swapaxes(v.reshape(v.shape[:-1] + (n, 128)), -1, -2)


_CACHE = {}


def kernel(x, c, ada_w, ada_b, norm1_g, w_in, cmp_pos_k, cmp_w1_k, cmp_w2_k, cmp_pos_v, cmp_w1_v, cmp_w2_v,
           lru_conv_w, lru_conv_b, lru_wa, lru_ba, lru_wi, lru_bi, lru_lambda, proj_a, proj_b, w_out, norm2_g,
           ffn_up, ffn_conv_w, ffn_conv_b, ffn_down, final_g, _cores=None, _dbg=False, _stop=None):
    f = lambda a: np.ascontiguousarray(np.asarray(a, dtype=np.float32))
    x = f(x)
    L = int(np.asarray(w_in).shape[0])
    B = x.shape[0]
    cores = list(range(B)) if _cores is None else list(_cores)
    colp = np.zeros((L, 128, NCP), np.float32)

    def putc(name, arr):
        o, w = CP[name]
        colp[:, :, o:o + w] = arr.reshape(L, 128, w)
    putc("adab", _col(f(ada_b), 96))
    putc("n1g", _col(f(norm1_g), 16))
    putc("n2g", _col(f(norm2_g), 16))
    putc("lcw", np.transpose(f(lru_conv_w).reshape(L, 4, 16, 128), (0, 3, 2, 1)))
    putc("lcb", _col(f(lru_conv_b), 16))
    putc("lba", _col(f(lru_ba), 16))
    putc("lbi", _col(f(lru_bi), 16))
    putc("llam", _col(f(lru_lambda), 16))
    putc("fcw", np.transpose(f(ffn_conv_w).reshape(L, 3, 48, 128), (0, 3, 2, 1)))
    putc("fcb", _col(f(ffn_conv_b), 48))
    putc("posk", np.transpose(f(cmp_pos_k), (0, 2, 1)))
    putc("posv", np.transpose(f(cmp_pos_v), (0, 2, 1)))
    cst0 = _consts()
    fg = _col(f(final_g), 16)
    shared = {"colp": colp, "ada_w": f(ada_w), "w_in": f(w_in), "cmp_w1_k": f(cmp_w1_k), "cmp_w2_k": f(cmp_w2_k),
              "cmp_w1_v": f(cmp_w1_v), "cmp_w2_v": f(cmp_w2_v), "lru_wa": f(lru_wa), "lru_wi": f(lru_wi),
              "proj_a": f(proj_a), "proj_b": f(proj_b), "w_out": f(w_out), "ffn_up": f(ffn_up), "ffn_down": f(ffn_down)}
    in_maps = []
    cc = f(c)
    for b in cores:
        cst = cst0.copy()
        o, w = CS["cT"]
        cst[:, o:o + w] = _col(cc[b], 16)
        o, w = CS["fing"]
        cst[:, o:o + w] = fg
        m = dict(shared)
        m["x"] = x[b]
        m["cst"] = cst
        in_maps.append(m)
    key = (L, _dbg, _stop)
    if key not in _CACHE:
        _CACHE[key] = build(L, _dbg, _stop)[0]
    nc = _CACHE[key]
    res = run_bass_kernel_spmd(nc, in_maps, core_ids=list(range(len(cores))))
    if _dbg:
        return res.results
    return np.stack([np.asarray(r["out"], dtype=np.float32) for r in res.results], axis=0)
```

```python
import os
import numpy as np
from contextlib import ExitStack
import concourse.bass as bass
import concourse.mybir as mybir
from concourse.bass_utils import run_bass_kernel_spmd

F32 = mybir.dt.float32
BF16 = mybir.dt.bfloat16
AF = mybir.ActivationFunctionType
ALU = mybir.AluOpType
AX = mybir.AxisListType

T = 2048
D = 2048
KC = 16
DFF = 6144
INW = 13360
NEG = -30000.0

CP = {}
_o = 0
for _n, _w in (("adab", 96), ("n1g", 16), ("n2g", 16), ("lcw", 64), ("lcb", 16), ("lba", 16), ("lbi", 16),
               ("llam", 16), ("fcw", 144), ("fcb", 48), ("posk", 32), ("posv", 32)):
    CP[_n] = (_o, _w)
    _o += _w
NCP = _o
CS = {}
_o = 0
for _n, _w in (("cT", 16), ("fing", 16), ("rowv", 16), ("cmpb", 2048), ("winb", 640), ("validm", 512),
               ("addc", 512), ("ovl", 32), ("ident", 128)):
    CS[_n] = (_o, _w)
    _o += _w
NCS = _o


class Buf:
    __slots__ = ("writer", "readers")

    def __init__(self):
        self.writer = None
        self.readers = {}


class StopBuild(Exception):
    pass


class Prog:
    stop_at = None
    ENGS = ("pe", "act", "dve", "pool", "sp")
    NSLOT = 8

    def __init__(self, nc):
        self.nc = nc
        self.es = ExitStack()
        self.q = {e: [] for e in self.ENGS}
        self.cnt = {e: 0 for e in self.ENGS}
        self.seen = {e: {} for e in self.ENGS}
        self.sems = {}
        for e in ("pe", "act", "dve", "pool"):
            self.sems["c_" + e] = self.es.enter_context(nc.semaphore("c_" + e))
        self.dma_i = {}
        self.dma_tgt = {}
        for e in ("sp", "pool", "act"):
            self.dma_i[e] = 0
            for s in range(self.NSLOT):
                k = "d_%s_%d" % (e, s)
                self.sems[k] = self.es.enter_context(nc.semaphore(k))
                self.dma_tgt[k] = 0
        self.out_toks = []
        self.n_ins = 0
        self.n_t = 0
        self.ps = None

    skip_set = ()
    skip = False

    def phase_begin(self):
        self.ps = ExitStack()
        self.skip = getattr(self, "n_phase", 0) in self.skip_set

    def tile(self, name, shape, dt, psum=False):
        st = self.ps if self.ps is not None else self.es
        self.n_t += 1
        name = "%s_t%d" % (name, self.n_t)
        if psum:
            return st.enter_context(self.nc.psum_tensor(name, list(shape), dt))
        return st.enter_context(self.nc.sbuf_tensor(name, list(shape), dt))

    def barrier(self):
        for e in self.ENGS:
            seen = self.seen[e]
            for f in ("pe", "act", "dve", "pool"):
                k = "c_" + f
                if f != e and self.cnt[f] > 0 and seen.get(k, 0) < self.cnt[f]:
                    self.q[e].append(("wait", k, self.cnt[f]))
                    seen[k] = self.cnt[f]
            for k, v in self.dma_tgt.items():
                if v > 0 and seen.get(k, 0) < v:
                    self.q[e].append(("wait", k, v))
                    seen[k] = v

    def phase_end(self):
        self.skip = False
        self.barrier()
        self.flush()
        self.ps.close()
        self.ps = None
        self.n_phase = getattr(self, "n_phase", 0) + 1
        if self.stop_at is not None and self.n_phase >= self.stop_at:
            self.finish()
            raise StopBuild(self)

    def flush(self):
        nc = self.nc
        handles = {"pe": "tensor", "act": "scalar", "dve": "vector", "pool": "gpsimd", "sp": "sync"}
        with nc.Block() as block:
            for e in self.ENGS:
                q = self.q[e]
                sems = self.sems

                def body(eng, q=q, sems=sems):
                    for it in q:
                        if it[0] == "wait":
                            eng.wait_ge(sems[it[1]], it[2])
                        else:
                            ins = it[1](eng)
                            if it[2] is not None:
                                ins.then_inc(sems[it[2]], it[3])
                getattr(block, handles[e])(body)
        tot = getattr(self, "tot", {e: 0 for e in self.ENGS})
        for e in self.ENGS:
            tot[e] += len(self.q[e])
        self.tot = tot
        self.q = {e: [] for e in self.ENGS}

    def _deps(self, eng, reads, writes):
        deps = []
        for b in reads:
            if b.writer is not None:
                deps.append(b.writer)
        for b in writes:
            deps.extend(b.readers.values())
            if b.writer is not None:
                deps.append(b.writer)
        seen = self.seen[eng]
        for (k, v, src) in deps:
            if src == eng and eng == "pe":
                continue
            if seen.get(k, 0) >= v:
                continue
            self.q[eng].append(("wait", k, v))
            seen[k] = v

    def _mark(self, tok, reads, writes):
        for b in reads:
            b.readers[tok[0]] = tok
        for b in writes:
            b.writer = tok
            b.readers = {}

    def op(self, eng, fn, reads=(), writes=(), inc=True):
        if self.skip:
            return None
        self._deps(eng, reads, writes)
        k = "c_" + eng
        if inc:
            self.cnt[eng] += 1
            tok = (k, self.cnt[eng], eng)
        else:
            tok = (k, self.cnt[eng] + 1, eng)
        self.q[eng].append(("ins", fn, k if inc else None, 1))
        self._mark(tok, reads, writes)
        self.n_ins += 1
        return tok

    def dma(self, qe, out, in_, reads=(), writes=(), is_output=False, **kw):
        if self.skip:
            return None
        i = self.dma_i[qe]
        self.dma_i[qe] = i + 1
        k = "d_%s_%d" % (qe, i % self.NSLOT)
        prev = self.dma_tgt[k]
        seen = self.seen[qe]
        if prev > 0 and seen.get(k, 0) < prev:
            self.q[qe].append(("wait", k, prev))
            seen[k] = prev
        self._deps(qe, reads, writes)
        tgt = prev + 16
        self.dma_tgt[k] = tgt
        tok = (k, tgt, "dma")

        def fn(e, out=out, in_=in_, kw=kw):
            return e.dma_start(out=out, in_=in_, **kw)
        self.q[qe].append(("ins", fn, k, 16))
        self._mark(tok, reads, writes)
        if is_output:
            self.out_toks.append(tok)
        self.n_ins += 1
        return tok

    def finish(self):
        for (k, v, _) in self.out_toks:
            if self.seen["sp"].get(k, 0) < v:
                self.q["sp"].append(("wait", k, v))
                self.seen["sp"][k] = v
        self.flush()
        self.es.close()

    def mm(self, out, lhsT, rhs, start, stop, reads, writes, inc=None):
        self.op("pe", lambda e: e.matmul(out, lhsT=lhsT, rhs=rhs, start=start, stop=stop), reads, writes,
                inc=stop if inc is None else inc)

    def tr(self, out, in_, ident, reads, writes, inc=True):
        self.op("pe", lambda e: e.transpose(out=out, in_=in_, identity=ident), reads, writes, inc=inc)

    def act(self, out, in_, func, reads, writes, **kw):
        self.op("act", lambda e: e.activation(out=out, in_=in_, func=func, **kw), reads, writes)

    def tt(self, out, a, b, op, reads, writes, eng="dve"):
        self.op(eng, lambda e: e.tensor_tensor(out=out, in0=a, in1=b, op=op), reads, writes)

    def ts(self, out, a, s1, s2, op0, op1, reads, writes, eng="dve"):
        self.op(eng, lambda e: e.tensor_scalar(out=out, in0=a, scalar1=s1, scalar2=s2, op0=op0, op1=op1), reads, writes)

    def stt(self, out, a, s, b, op0, op1, reads, writes):
        self.op("dve", lambda e: e.scalar_tensor_tensor(out=out, in0=a, scalar=s, in1=b, op0=op0, op1=op1), reads, writes)

    def gelu(self, out, x, tm, rx, btm, wout, eng2="pool"):
        self.act(tm, x, AF.Square, rx, [btm])
        self.ts(tm, tm, 0.044715, 1.0, ALU.mult, ALU.add, [btm], [btm], eng=eng2)
        self.tt(tm, tm, x, ALU.mult, [btm] + rx, [btm])
        self.act(tm, tm, AF.Sigmoid, [btm], [btm], scale=1.5957691216057308)
        if out is not None:
            self.tt(out, tm, x, ALU.mult, [btm] + rx, wout)

    def cp(self, out, in_, reads, writes, eng="dve"):
        if eng == "act":
            self.act(out, in_, AF.Copy, reads, writes)
        else:
            self.op(eng, lambda e: e.tensor_copy(out=out, in_=in_), reads, writes)


def build(L, dbg=False, stop=None):
    nc = bass.Bass("TRN2", target_bir_lowering=False)
    Prog.stop_at = stop
    import os
    Prog.skip_set = tuple(int(v) for v in os.environ.get("KSKIP", "").split(",") if v)
    try:
        return _build(nc, L, dbg)
    except StopBuild as e:
        P = e.args[0] if e.args else None
        return nc, P


def _build(nc, L, dbg):

    def din(name, shape):
        return nc.dram_tensor(name, list(shape), F32, kind="ExternalInput").ap()

    x_in = din("x", [T, D])
    cst_in = din("cst", [128, NCS])
    colp_in = din("colp", [L, 128, NCP])
    ada_w = din("ada_w", [L, D, 6 * D])
    w_in = din("w_in", [L, D, INW])
    w1k = din("cmp_w1_k", [L, 32, 128, 128])
    w2k = din("cmp_w2_k", [L, 128, 128])
    w1v = din("cmp_w1_v", [L, 32, 128, 128])
    w2v = din("cmp_w2_v", [L, 128, 128])
    lru_wa = din("lru_wa", [L, 16, 128, 128])
    lru_wi = din("lru_wi", [L, 16, 128, 128])
    proj_a = din("proj_a", [L, D, D])
    proj_b = din("proj_b", [L, D, D])
    w_out = din("w_out", [L, D, D])
    ffn_up = din("ffn_up", [L, D, 2 * DFF])
    ffn_down = din("ffn_down", [L, DFF, D])
    out = nc.dram_tensor("out", [T, D], F32, kind="ExternalOutput").ap()
    sk = "ExternalOutput" if dbg else "Internal"

    def dsc(name, shape, dt):
        return nc.dram_tensor(name, list(shape), dt, kind=sk).ap()

    xT = dsc("s_xT", [D, T], F32)
    qT = dsc("s_qT", [D, T], BF16)
    kcT = dsc("s_kcT", [512, T], BF16)
    vcT = dsc("s_vcT", [512, T], BF16)
    ksT = dsc("s_ksT", [512, T], BF16)
    kwT = dsc("s_kwT", [512, T], BF16)
    vsd = dsc("s_vs", [T, 512], BF16)
    vwd = dsc("s_vw", [T, 512], BF16)
    gtd = dsc("s_gates", [T, 48], F32)
    lxT = dsc("s_lxT", [D, T], F32)
    glyT = dsc("s_glyT", [D, T], BF16)
    sgaT = dsc("s_sgaT", [D, T], BF16)
    sgbT = dsc("s_sgbT", [D, T], BF16)
    obT = dsc("s_obT", [D, T], BF16)
    oaT = dsc("s_oaT", [D, T], BF16)
    hidT = dsc("s_hidT", [DFF, T], BF16)

    P = Prog(nc)
    cst = P.tile("cst", [128, NCS], F32)
    colp = P.tile("colp", [128, L, NCP], F32)
    modc = P.tile("modc", [128, L, 96], F32)
    A1 = P.tile("A1", [128, L, 16], F32)
    A2 = P.tile("A2", [128, L, 16], F32)
    clam = P.tile("clam", [128, L, 16], F32)
    clam2 = P.tile("clam2", [128, L, 16], F32)
    cact = P.tile("cact", [128, 16], F32)
    identb = P.tile("identb", [128, 128], BF16)
    ovlb = P.tile("ovlb", [128, 32], BF16)
    onesf = P.tile("onesf", [128, 128], F32)
    arena = P.tile("arena", [128, 49152], BF16)
    hT = arena[:, 0:32768].rearrange("p (k t) -> p k t", t=T)
    arena_f = arena[:].bitcast(F32)

    def AR(o, n):
        return arena_f[:, o:o + n]

    def ARB(o, n):
        return arena[:, o:o + n]
    bH = [Buf() for _ in range(KC)]

    def C(name, a=0, b=None):
        o, w = CS[name]
        return cst[:, o + a:o + (w if b is None else b)]

    def CPc(l, name, a=0, b=None):
        o, w = CP[name]
        return colp[:, l, o + a:o + (w if b is None else b)]

    ident_f = C("ident")

    P.phase_begin()
    bC = Buf()
    P.dma("sp", cst[:], cst_in, writes=[bC])
    P.dma("sp", colp[:], colp_in.rearrange("l p n -> p l n"), writes=[bC])
    P.cp(identb[:], C("ident"), [bC], [bC])
    P.cp(ovlb[:], C("ovl"), [bC], [bC])
    P.op("dve", lambda e: e.memset(onesf[:], 1.0), [], [bC])
    e1 = P.tile("e1", [128, 16], F32)
    P.act(e1[:], C("cT"), AF.Exp, [bC], [bC], scale=-1.0)
    P.ts(e1[:], e1[:], 1.0, None, ALU.add, ALU.bypass, [bC], [bC])
    P.op("dve", lambda e: e.reciprocal(out=e1[:], in_=e1[:]), [bC], [bC])
    P.tt(cact[:], e1[:], C("cT"), ALU.mult, [bC], [bC])
    for l in range(L):
        P.act(clam[:, l, :], CPc(l, "llam"), AF.Exp, [bC], [bC], scale=-1.0)
    for l in range(L):
        P.act(clam[:, l, :], clam[:, l, :], AF.Ln, [bC], [bC], bias=1.0)
    for l in range(L):
        P.ts(clam2[:, l, :], clam[:, l, :], -16.0, None, ALU.mult, ALU.bypass, [bC], [bC])
        P.ts(clam[:, l, :], clam[:, l, :], -8.0, None, ALU.mult, ALU.bypass, [bC], [bC])
    NST = 2
    ast = [P.tile("ast%d" % i, [128, 16, 512], F32) for i in range(NST)]
    bast = [Buf() for _ in range(NST)]
    psm = P.tile("psm", [128, 96], F32, psum=True)
    bpsm = Buf()
    ci = 0
    for l in range(L):
        awv = ada_w[l].rearrange("(k p) m -> p k m", p=128)
        for ch in range(24):
            s = ci % NST
            ci += 1
            P.dma("sp", ast[s][:], awv[:, :, ch * 512:(ch + 1) * 512], writes=[bast[s]])
            for j in range(4):
                col = ch * 4 + j
                for k in range(KC):
                    P.mm(psm[:, col:col + 1], ast[s][:, k, j * 128:(j + 1) * 128], cact[:, k:k + 1],
                         k == 0, k == KC - 1, [bast[s], bC], [bpsm])
        P.tt(modc[:, l, :], psm[:], CPc(l, "adab"), ALU.add, [bpsm, bC], [bC, bpsm])
        P.ts(A1[:, l, :], modc[:, l, 16:32], 1.0, None, ALU.add, ALU.bypass, [bC], [bC])
        P.tt(A1[:, l, :], A1[:, l, :], CPc(l, "n1g"), ALU.mult, [bC], [bC])
        P.ts(A2[:, l, :], modc[:, l, 64:80], 1.0, None, ALU.add, ALU.bypass, [bC], [bC])
        P.tt(A2[:, l, :], A2[:, l, :], CPc(l, "n2g"), ALU.mult, [bC], [bC])
    P.phase_end()

    P.phase_begin()
    xt = [P.tile("xt%d" % i, [128, D], F32) for i in range(2)]
    bxt = [Buf() for _ in range(2)]
    xo = [P.tile("xo%d" % i, [128, 16, 128], F32) for i in range(2)]
    bxo = [Buf() for _ in range(2)]
    pst = [P.tile("pst%d" % i, [128, 4, 128], F32, psum=True) for i in range(4)]
    bpst = [Buf() for _ in range(4)]
    pi = 0
    xTv = xT.rearrange("(k p) t -> p k t", p=128)
    for tt_ in range(16):
        s = tt_ % 2
        P.dma("sp", xt[s][:], x_in[tt_ * 128:(tt_ + 1) * 128, :], writes=[bxt[s]])
        for g4 in range(4):
            pp = pi % 4
            pi += 1
            for j in range(4):
                fc = g4 * 4 + j
                P.tr(pst[pp][:, j, :], xt[s][:, fc * 128:(fc + 1) * 128], ident_f, [bxt[s]], [bpst[pp]], inc=(j == 3))
            P.cp(xo[s][:, g4 * 4:(g4 + 1) * 4, :], pst[pp][:], [bpst[pp]], [bxo[s]], eng=("act" if g4 % 2 else "dve"))
        P.dma("sp", xTv[:, :, tt_ * 128:(tt_ + 1) * 128], xo[s][:], reads=[bxo[s]])
    P.phase_end()

    def norm_phase(Acol, Bcol, final=False):
        P.phase_begin()
        NW = 256
        xc = [P.tile("xc%d" % i, [128, 16, NW], F32) for i in range(2)]
        bxc = [Buf() for _ in range(2)]
        sq = [P.tile("sq%d" % i, [128, NW], F32) for i in range(3)]
        bsq = [Buf() for _ in range(3)]
        pss = P.tile("pss", [128, 512], F32, psum=True)
        bpss = Buf()
        rinv = P.tile("rinv", [128, NW], F32)
        brinv = Buf()
        t1 = [P.tile("t1%d" % i, [128, NW], F32) for i in range(3)]
        bt1 = [Buf() for _ in range(3)]
        if final:
            ot = [P.tile("ot%d" % i, [128, D], F32) for i in range(4)]
            bot = [Buf() for _ in range(4)]
            psf = [P.tile("psf%d" % i, [128, 4, 128], F32, psum=True) for i in range(2)]
            bpsf = [Buf() for _ in range(2)]
        for n in range(T // NW):
            s = n % 2
            P.dma("sp", xc[s][:], xTv[:, :, n * NW:(n + 1) * NW], writes=[bxc[s]])
            for fc in range(KC):
                q3 = fc % 3
                P.act(sq[q3][:], xc[s][:, fc, :], AF.Square, [bxc[s]], [bsq[q3]])
                P.mm(pss[:, 0:NW], onesf[:], sq[q3][:], fc == 0, fc == KC - 1, [bsq[q3]], [bpss], inc=True)
            P.act(rinv[:], pss[:, 0:NW], AF.Sqrt, [bpss], [brinv], bias=1e-6, scale=1.0 / D)
            P.op("dve", lambda e: e.reciprocal(out=rinv[:], in_=rinv[:]), [brinv], [brinv])
            for fc in range(KC):
                q3 = fc % 3
                P.tt(t1[q3][:], xc[s][:, fc, :], rinv[:], ALU.mult, [bxc[s], brinv], [bt1[q3]])
                if not final:
                    P.act(hT[:, fc, n * NW:(n + 1) * NW], t1[q3][:], AF.Identity, [bt1[q3]], [bH[fc]],
                          scale=Acol[:, fc:fc + 1], bias=Bcol[:, fc:fc + 1])
                else:
                    P.act(t1[q3][:], t1[q3][:], AF.Identity, [bt1[q3]], [bt1[q3]], scale=Acol[:, fc:fc + 1])
                    pf = fc % 2
                    for j in range(2):
                        P.tr(psf[pf][:, j, :], t1[q3][:, j * 128:(j + 1) * 128], ident_f, [bt1[q3]], [bpsf[pf]], inc=(j == 1))
                    for j in range(2):
                        oj = (n % 2) * 2 + j
                        P.cp(ot[oj][:, fc * 128:(fc + 1) * 128], psf[pf][:, j, :], [bpsf[pf]], [bot[oj]],
                             eng=("act" if pf else "dve"))
            if final:
                for j in range(2):
                    oj = (n % 2) * 2 + j
                    r0 = n * NW + j * 128
                    P.dma("sp", out[r0:r0 + 128, :], ot[oj][:], reads=[bot[oj]], is_output=True)
        P.phase_end()

    class WStream:
        def __init__(self, nb=3, width=8192):
            self.t = [P.tile("wb%d" % i, [128, width], BF16) for i in range(nb)]
            self.b = [Buf() for _ in range(nb)]
            self.i = 0
            self.nb = nb

        def load(self, wsrc, kc, ncols):
            s = self.i % self.nb
            self.i += 1
            v = self.t[s][:, 0:kc * ncols].rearrange("p (k m) -> p k m", m=ncols)
            P.dma("pool", v, wsrc.rearrange("(k p) m -> p k m", p=128), writes=[self.b[s]])
            return v, self.b[s]

    for l in range(L):
        sh1, g1c = modc[:, l, 0:16], modc[:, l, 32:48]
        sh2, g2c = modc[:, l, 48:64], modc[:, l, 80:96]
        norm_phase(A1[:, l, :], sh1)

        P.phase_begin()
        ws = WStream()
        psz = [P.tile("psz%d" % i, [128, T], F32, psum=True) for i in range(2)]
        bpsz = [Buf() for _ in range(2)]
        ob16 = [P.tile("ob16_%d" % i, [128, T], BF16) for i in range(3)]
        bob16 = [Buf() for _ in range(3)]
        of32 = [P.tile("of32_%d" % i, [128, T], F32) for i in range(2)]
        bof32 = [Buf() for _ in range(2)]
        cnt = {"p": 0, "o": 0, "f": 0}

        def fm_chunk(col0, kind, dst, row0):
            wv, bw = ws.load(w_in[l][:, col0:col0 + 512], KC, 512)
            for m in range(4):
                pp = cnt["p"] % 2
                cnt["p"] += 1
                for k in range(KC):
                    for n in range(4):
                        P.mm(psz[pp][:, n * 512:(n + 1) * 512], wv[:, k, m * 128:(m + 1) * 128],
                             hT[:, k, n * 512:(n + 1) * 512], k == 0, k == KC - 1, [bw, bH[k]], [bpsz[pp]],
                             inc=(k == KC - 1 and n == 3))
                r = row0 + m * 128
                if kind == "lx":
                    o = cnt["f"] % 2
                    cnt["f"] += 1
                    P.cp(of32[o][:], psz[pp][:], [bpsz[pp]], [bof32[o]], eng="dve")
                    P.dma("sp", dst[r:r + 128, :], of32[o][:], reads=[bof32[o]])
                else:
                    o = cnt["o"] % 3
                    cnt["o"] += 1
                    if kind == "q":
                        P.act(ob16[o][:], psz[pp][:], AF.Copy, [bpsz[pp]], [bob16[o]], scale=float(128.0 ** -0.5))
                    elif kind == "copy":
                        P.cp(ob16[o][:], psz[pp][:], [bpsz[pp]], [bob16[o]], eng="dve")
                    elif kind == "gelu":
                        o2 = cnt["f"] % 2
                        cnt["f"] += 1
                        P.gelu(ob16[o][:], psz[pp][:], of32[o2][:], [bpsz[pp]], bof32[o2], [bob16[o]])
                    elif kind == "sig":
                        P.act(ob16[o][:], psz[pp][:], AF.Sigmoid, [bpsz[pp]], [bob16[o]])
                    P.dma("sp", dst[r:r + 128, :], ob16[o][:], reads=[bob16[o]])

        def tm_chunk(col0, ncols, dst, f32out):
            wv, bw = ws.load(w_in[l][:, col0:col0 + ncols], KC, ncols)
            for tt_ in range(16):
                pp = cnt["p"] % 2
                cnt["p"] += 1
                for k in range(KC):
                    P.mm(psz[pp][:, 0:ncols], hT[:, k, tt_ * 128:(tt_ + 1) * 128], wv[:, k, :],
                         k == 0, k == KC - 1, [bw, bH[k]], [bpsz[pp]])
                if f32out:
                    o = cnt["f"] % 2
                    cnt["f"] += 1
                    P.cp(of32[o][:, 0:ncols], psz[pp][:, 0:ncols], [bpsz[pp]], [bof32[o]], eng="dve")
                    P.dma("sp", dst[tt_ * 128:(tt_ + 1) * 128, :], of32[o][:, 0:ncols], reads=[bof32[o]])
                else:
                    o = cnt["o"] % 3
                    cnt["o"] += 1
                    P.cp(ob16[o][:, 0:ncols], psz[pp][:, 0:ncols], [bpsz[pp]], [bob16[o]], eng="act")
                    P.dma("sp", dst[tt_ * 128:(tt_ + 1) * 128, :], ob16[o][:, 0:ncols], reads=[bob16[o]])

        for i in range(4):
            fm_chunk(i * 512, "q", qT, i * 512)
        fm_chunk(2048, "copy", kcT, 0)
        fm_chunk(2560, "copy", vcT, 0)
        fm_chunk(3072, "copy", ksT, 0)
        tm_chunk(3584, 512, vsd, False)
        fm_chunk(4096, "copy", kwT, 0)
        tm_chunk(4608, 512, vwd, False)
        tm_chunk(5120, 48, gtd, True)
        for i in range(4):
            fm_chunk(5168 + i * 512, "lx", lxT, i * 512)
        for i in range(4):
            fm_chunk(7216 + i * 512, "gelu", glyT, i * 512)
        for i in range(4):
            fm_chunk(9264 + i * 512, "sig", sgaT, i * 512)
        for i in range(4):
            fm_chunk(11312 + i * 512, "sig", sgbT, i * 512)
        P.phase_end()

        P.phase_begin()
        wab = P.tile("wab", [128, 16, 128], BF16)
        wib = P.tile("wib", [128, 16, 128], BF16)
        bwg = Buf()
        P.dma("pool", wab[:], lru_wa[l].rearrange("n d e -> d n e"), writes=[bwg])
        P.dma("pool", wib[:], lru_wi[l].rearrange("n d e -> d n e"), writes=[bwg])
        xp = [P.tile("xp%d" % i, [128, T + 3], F32) for i in range(2)]
        bxp = [Buf() for _ in range(2)]
        gly = [P.tile("gly%d" % i, [128, T], BF16) for i in range(2)]
        bgly = [Buf() for _ in range(2)]
        for i in range(2):
            P.op("pool", lambda e, i=i: e.memset(xp[i][:, 0:3], 0.0), [], [bxp[i]])
        U = [AR(s_ * 10240 + 0, T) for s_ in range(2)]; bU = [Buf(), Buf()]
        RR = [AR(s_ * 10240 + 2048, T) for s_ in range(2)]; bRR = [Buf(), Buf()]
        II = [AR(s_ * 10240 + 4096, T) for s_ in range(2)]; bII = [Buf(), Buf()]
        AA = [AR(s_ * 10240 + 6144, T) for s_ in range(2)]; bAA = [Buf(), Buf()]
        MM = [AR(s_ * 10240 + 8192, T) for s_ in range(2)]; bMM = [Buf(), Buf()]
        ub = [P.tile("ub%d" % i, [128, T], BF16) for i in range(2)]
        bub = [Buf(), Buf()]
        obt = [P.tile("obt%d" % i, [128, T], BF16) for i in range(2)]
        bobt = [Buf() for _ in range(2)]
        psr = P.tile("psr", [128, T], F32, psum=True); bpsr = Buf()
        psi = P.tile("psi", [128, T], F32, psum=True); bpsi = Buf()

        def lru_a(ct):
            s = ct % 2
            u, rr, ii, aa, mmx = U[s], RR[s], II[s], AA[s], MM[s]
            P.dma("sp", xp[s][:, 3:T + 3], lxT[ct * 128:(ct + 1) * 128, :], writes=[bxp[s]])
            P.dma("sp", gly[s][:], glyT[ct * 128:(ct + 1) * 128, :], writes=[bgly[s]])
            lcw = CPc(l, "lcw", ct * 4, ct * 4 + 4)
            P.ts(u, xp[s][:, 3:T + 3], lcw[:, 3:4], CPc(l, "lcb", ct, ct + 1), ALU.mult, ALU.add, [bxp[s]], [bU[s]])
            for j in (2, 1, 0):
                P.stt(u, xp[s][:, j:j + T], lcw[:, j:j + 1], u, ALU.mult, ALU.add, [bxp[s], bU[s]], [bU[s]])
            P.cp(ub[s][:], u, [bU[s]], [bub[s]], eng="pool")
            for n in range(4):
                P.mm(psr[:, n * 512:(n + 1) * 512], wab[:, ct, :], ub[s][:, n * 512:(n + 1) * 512], True, True,
                     [bwg, bub[s]], [bpsr], inc=(n == 3))
            for n in range(4):
                P.mm(psi[:, n * 512:(n + 1) * 512], wib[:, ct, :], ub[s][:, n * 512:(n + 1) * 512], True, True,
                     [bwg, bub[s]], [bpsi], inc=(n == 3))
            P.act(rr, psr[:], AF.Sigmoid, [bpsr], [bRR[s]], bias=CPc(l, "lba", ct, ct + 1))
            P.act(ii, psi[:], AF.Sigmoid, [bpsi], [bII[s]], bias=CPc(l, "lbi", ct, ct + 1))
            P.act(aa, rr, AF.Exp, [bRR[s]], [bAA[s]], scale=clam[:, l, ct:ct + 1])
            P.act(mmx, rr, AF.Exp, [bRR[s]], [bMM[s]], scale=clam2[:, l, ct:ct + 1])
            P.ts(mmx, mmx, -1.0, 1.0, ALU.mult, ALU.add, [bMM[s]], [bMM[s]], eng="pool")
            P.ts(mmx, mmx, 1e-20, None, ALU.max, ALU.bypass, [bMM[s]], [bMM[s]], eng="pool")
            P.act(mmx, mmx, AF.Sqrt, [bMM[s]], [bMM[s]])

        def lru_b(ct):
            s = ct % 2
            u, rr, ii, aa, mmx = U[s], RR[s], II[s], AA[s], MM[s]
            P.tt(ii, ii, u, ALU.mult, [bII[s], bU[s]], [bII[s]])
            P.tt(ii, ii, mmx, ALU.mult, [bII[s], bMM[s]], [bII[s]])
            P.op("dve", lambda e: e.tensor_tensor_scan(out=rr, data0=aa, data1=ii, initial=0.0,
                                                       op0=ALU.mult, op1=ALU.add), [bAA[s], bII[s], bRR[s]], [bRR[s]])
            P.tt(obt[s][:], rr, gly[s][:], ALU.mult, [bRR[s], bgly[s]], [bobt[s]])
            P.dma("sp", obT[ct * 128:(ct + 1) * 128, :], obt[s][:], reads=[bobt[s]])

        lru_a(0)
        for ct in range(16):
            if ct + 1 < 16:
                lru_a(ct + 1)
            lru_b(ct)
        P.phase_end()

        P.phase_begin()
        gts = P.tile("gts", [128, 16, 48], F32); bg = Buf()
        P.dma("sp", gts[:], gtd.rearrange("(tt p) c -> p tt c", p=128), writes=[bg])
        P.act(gts[:], gts[:], AF.Sigmoid, [bg], [bg])
        w1b = [P.tile("w1b%d" % i, [128, 32, 128], BF16) for i in range(2)]
        w2b = [P.tile("w2b%d" % i, [128, 128], BF16) for i in range(2)]
        posb = [P.tile("posb%d" % i, [128, 32], BF16) for i in range(2)]
        bw1 = Buf()
        for i, (w1, w2, pn) in enumerate(((w1k, w2k, "posk"), (w1v, w2v, "posv"))):
            P.dma("pool", w1b[i][:], w1[l].rearrange("s d e -> d s e"), writes=[bw1])
            P.dma("pool", w2b[i][:], w2[l], writes=[bw1])
            P.cp(posb[i][:], CPc(l, pn), [], [bw1])
        ckk = P.tile("ckk", [128, 2], F32); bck = Buf()
        kct = ARB(24576, T); vct = ARB(26624, T); bkv = Buf()
        kst = ARB(28672, T); kwt = ARB(30720, T); bks = Buf()
        vst = P.tile("vst", [128, 16, 128], BF16); vwt = P.tile("vwt", [128, 16, 128], BF16); bvs = Buf()
        qt = [ARB(i * T, T) for i in range(4)]
        bq = [Buf() for _ in range(4)]
        hid = P.tile("hid", [128, 128], BF16); bhid = Buf()
        xh = P.tile("xh", [128, 128], F32); bxh = Buf()
        th_ = P.tile("th_", [128, 128], F32); bth = Buf()
        kcmp = P.tile("kcmp", [128, 128], BF16); vcmp = P.tile("vcmp", [128, 128], BF16); bcmp = Buf()
        oast3 = ARB(8192, 4 * T).rearrange("p (h t) -> p h t", t=T)
        oast = [ARB(8192 + i * T, T) for i in range(4)]
        boast = Buf()
        oacc3 = [P.tile("oacc%d" % i, [128, 4, 128], F32) for i in range(2)]
        boacc = [[Buf() for _ in range(4)] for _ in range(2)]
        obf4 = P.tile("obf4", [128, 4, 128], BF16); bobf = Buf()
        biasv = [AR(10240, T), AR(19712, T)]
        bbias = [Buf(), Buf()]
        biasf = [b_.rearrange("p (a b) -> p a b", b=64) for b_ in biasv]
        smc = AR(21760, 512); bsmc = Buf()
        smc3 = smc.rearrange("p (h n) -> p h n", n=128)
        pbc = ARB(44544, 512); bpbc = Buf()
        pbc3 = pbc.rearrange("p (h n) -> p h n", n=128)
        pTc = ARB(45056, 512).rearrange("p (h n) -> p h n", n=128); bpTc = Buf()
        smlc = P.tile("smlc", [128, 16], F32); bsmlc = Buf()
        sc = P.tile("sc", [128, 32], F32); sc2 = P.tile("sc2", [128, 32], F32); bsc = Buf()
        m8 = P.tile("m8", [128, 16], F32)
        selb = P.tile("selb", [128, 32], F32)
        ps_m = P.tile("ps_m", [128, 512], F32, psum=True); bpm = Buf()
        ps_m3 = ps_m[:].rearrange("p (h n) -> p h n", n=128)
        ps_s = P.tile("ps_s", [128, T], F32, psum=True); bps = Buf()
        ps_w = P.tile("ps_w", [128, 1024], F32, psum=True); bpw = Buf()
        ps_imp = ps_w[:, 0:32]
        ps_t = P.tile("ps_t", [128, 8, 128], BF16, psum=True); bpt = Buf()
        lanes = [
            dict(ps=ps_s, bps=bps, sm=AR(8192, T), bsm=Buf(), pb=ARB(32768, T), bpb=Buf(),
                 pT=ARB(34816, T).rearrange("p (k q) -> p k q", q=128), bpT=Buf(),
                 sml=P.tile("sml_s", [128, 4], F32), bsml=Buf(), tps=ps_t, btps=bpt, tcap=8, pv=ps_m[:, 0:128], bpv=bpm),
            dict(ps=ps_w, bps=bpw, sm=AR(18432, 640), bsm=Buf(), pb=ARB(38144, 640), bpb=Buf(),
                 pT=ARB(38784, 640).rearrange("p (k q) -> p k q", q=128), bpT=Buf(),
                 sml=P.tile("sml_w", [128, 4], F32), bsml=Buf(),
                 tps=ps_w[:, 768:1024].bitcast(BF16).rearrange("p (k q) -> p k q", q=128), btps=bpw, tcap=4,
                 pv=ps_w[:, 640:768], bpv=bpw),
        ]
        P.op("dve", lambda e: e.memset(hid[:], 0.0), [], [bhid])
        P.op("dve", lambda e: e.memset(kcmp[:], 0.0), [], [bcmp])
        P.op("dve", lambda e: e.memset(vcmp[:], 0.0), [], [bcmp])

        def chain(L_, h, i, nk, bias_ap, bias_bufs, kT, k0, vT, vb0, gcol):
            par = i % 2
            qs = qt[h][:, i * 128:(i + 1) * 128]
            sml = L_["sml"]
            nch = (nk + 511) // 512
            for c in range(nch):
                w = min(512, nk - c * 512)
                P.mm(L_["ps"][:, c * 512:c * 512 + w], qs, kT[:, k0 + c * 512:k0 + c * 512 + w], True, True,
                     [bq[h], bks], [L_["bps"]], inc=(c == nch - 1))
            yield
            P.tt(L_["sm"][:, 0:nk], L_["ps"][:, 0:nk], bias_ap, ALU.add, [L_["bps"]] + bias_bufs, [L_["bsm"]])
            P.op("dve", lambda e: e.reduce_max(out=sml[:, 0:1], in_=L_["sm"][:, 0:nk], axis=AX.X, negate=True),
                 [L_["bsm"]], [L_["bsml"]])
            yield
            P.act(L_["pb"][:, 0:nk], L_["sm"][:, 0:nk], AF.Exp, [L_["bsm"], L_["bsml"]], [L_["bpb"], L_["bsml"]],
                  bias=sml[:, 0:1], accum_out=sml[:, 1:2])
            yield
            nb = nk // 128
            tcap = L_["tcap"]
            for r0 in range(0, nb, tcap):
                nr = min(tcap, nb - r0)
                for kb in range(nr):
                    P.tr(L_["tps"][:, kb, :], L_["pb"][:, (r0 + kb) * 128:(r0 + kb + 1) * 128], identb[:], [L_["bpb"]],
                         [L_["btps"]], inc=(kb == nr - 1))
                P.cp(L_["pT"][:, r0:r0 + nr, :], L_["tps"][:, 0:nr, :], [L_["btps"]], [L_["bpT"]], eng="act")
                yield
            for kb in range(nb):
                P.mm(L_["pv"], L_["pT"][:, kb, :], vT[:, vb0 + kb, :], kb == 0, kb == nb - 1, [L_["bpT"], bvs], [L_["bpv"]])
            P.op("dve", lambda e: e.reciprocal(out=sml[:, 2:3], in_=sml[:, 1:2]), [L_["bsml"]], [L_["bsml"]])
            P.tt(sml[:, 3:4], sml[:, 2:3], gts[:, i, gcol:gcol + 1], ALU.mult, [L_["bsml"], bg], [L_["bsml"]])
            P.stt(oacc3[par][:, h, :], L_["pv"], sml[:, 3:4], oacc3[par][:, h, :], ALU.mult, ALU.add,
                  [L_["bpv"], L_["bsml"], boacc[par][h]], [boacc[par][h]])
            yield

        def lane_slc(g, i):
            for h in range(4):
                hd = g * 4 + h
                yield from chain(lanes[0], h, i, 128 * (i + 1), biasv[i % 2][:, 0:128 * (i + 1)], [bbias[i % 2]],
                                 kst, 0, vst, 0, hd * 3 + 1)

        def lane_win(g, i):
            j0 = max(0, i - 4)
            nkw = (i - j0 + 1) * 128
            for h in range(4):
                hd = g * 4 + h
                yield from chain(lanes[1], h, i, nkw, C("winb", 640 - nkw, 640), [], kwt, j0 * 128, vwt, j0, hd * 3 + 2)

        for g in range(4):
            P.dma("sp", kct[:], kcT[g * 128:(g + 1) * 128, :], writes=[bkv])
            P.dma("sp", vct[:], vcT[g * 128:(g + 1) * 128, :], writes=[bkv])
            P.dma("sp", kst[:], ksT[g * 128:(g + 1) * 128, :], writes=[bks])
            P.dma("sp", kwt[:], kwT[g * 128:(g + 1) * 128, :], writes=[bks])
            P.dma("sp", vst[:], vsd[:, g * 128:(g + 1) * 128].rearrange("(tt p) d -> p tt d", p=128), writes=[bvs])
            P.dma("sp", vwt[:], vwd[:, g * 128:(g + 1) * 128].rearrange("(tt p) d -> p tt d", p=128), writes=[bvs])
            for h in range(4):
                hd = g * 4 + h
                P.dma("sp", qt[h][:], qT[hd * 128:(hd + 1) * 128, :], writes=[bq[h]])
            for i, src in enumerate((kct, vct)):
                for s_ in range(32):
                    P.mm(ps_m[:, 256 + i:257 + i], w1b[i][:, s_, :], posb[i][:, s_:s_ + 1], s_ == 0, s_ == 31, [bw1], [bpm])
                P.cp(ckk[:, i:i + 1], ps_m[:, 256 + i:257 + i], [bpm], [bck])
                for s_ in range(32):
                    P.mm(ps_m[:, 0:127], w1b[i][:, s_, :], src[:, s_:s_ + 2017:16], s_ == 0, s_ == 31, [bw1, bkv], [bpm])
                P.act(xh[:, 0:127], ps_m[:, 0:127], AF.Identity, [bpm, bck], [bxh], bias=ckk[:, i:i + 1])
                P.gelu(hid[:, 0:127], xh[:, 0:127], th_[:, 0:127], [bxh], bth, [bhid])
                if i == 0:
                    P.mm(ps_m[:, 128:255], w2b[0][:], hid[:, 0:127], True, True, [bw1, bhid], [bpm])
                    P.cp(kcmp[:, 0:127], ps_m[:, 128:255], [bpm], [bcmp])
                else:
                    P.mm(ps_m[0:127, 128:256], hid[:, 0:127], w2b[1][:], True, True, [bw1, bhid], [bpm])
                    P.cp(vcmp[0:127, :], ps_m[0:127, 128:256], [bpm], [bcmp])
            for i in range(16):
                par = i % 2
                qc = slice(i * 128, (i + 1) * 128)
                for h in range(4):
                    P.mm(ps_m[:, h * 128:(h + 1) * 128], qt[h][:, qc], kcmp[:], True, True, [bq[h], bcmp], [bpm], inc=(h == 3))
                P.tt(smc3, ps_m3, C("cmpb", i * 128, (i + 1) * 128).unsqueeze(1).broadcast_to([128, 4, 128]), ALU.add,
                     [bpm], [bsmc])
                P.op("dve", lambda e: e.tensor_reduce(out=smlc[:, 0:4], in_=smc3, axis=AX.X, op=ALU.max), [bsmc], [bsmlc])
                P.tt(smc3, smc3, smlc[:, 0:4].unsqueeze(2).broadcast_to([128, 4, 128]), ALU.subtract, [bsmc, bsmlc], [bsmc])
                P.act(smc, smc, AF.Exp, [bsmc], [bsmc])
                P.op("dve", lambda e: e.tensor_reduce(out=smlc[:, 4:8], in_=smc3, axis=AX.X, op=ALU.add), [bsmc], [bsmlc])
                P.ts(smlc[:, 4:8], smlc[:, 4:8], 1e-30, None, ALU.max, ALU.bypass, [bsmlc], [bsmlc])
                P.op("dve", lambda e: e.reciprocal(out=smlc[:, 8:12], in_=smlc[:, 4:8]), [bsmlc], [bsmlc])
                P.ts(smlc[:, 8:12], smlc[:, 8:12], C("rowv", i, i + 1), None, ALU.mult, ALU.bypass, [bsmlc], [bsmlc])
                P.tt(pbc3, smc3, smlc[:, 8:12].unsqueeze(2).broadcast_to([128, 4, 128]), ALU.mult, [bsmc, bsmlc], [bpbc])
                for h in range(4):
                    P.tr(ps_t[:, h, :], pbc[:, h * 128:(h + 1) * 128], identb[:], [bpbc], [bpt], inc=(h == 3))
                P.cp(pTc[:, 0:4, :], ps_t[:, 0:4, :], [bpt], [bpTc], eng="act")
                for h in range(4):
                    P.mm(ps_m[:, h * 128:(h + 1) * 128], pTc[0:127, h, :], vcmp[0:127, :], True, True, [bpTc, bcmp], [bpm],
                         inc=(h == 3))
                for h in range(4):
                    P.mm(ps_imp, pTc[0:127, h, :], ovlb[0:127, :], h == 0, h == 3, [bpTc], [bpw])
                P.tt(oacc3[par][:], ps_m3, gts[:, i, g * 12:(g + 1) * 12:3].unsqueeze(2).broadcast_to([128, 4, 128]), ALU.mult,
                     [bpm, bg], boacc[par])
                P.tt(sc[:], ps_imp, C("validm", i * 32, (i + 1) * 32), ALU.mult, [bpw], [bsc])
                P.tt(sc[:], sc[:], C("addc", i * 32, (i + 1) * 32), ALU.add, [bsc], [bsc])
                P.op("dve", lambda e: e.max(out=m8[:, 0:8], in_=sc[:]), [bsc], [bsc])
                P.op("dve", lambda e: e.match_replace(out=sc2[:], in_to_replace=m8[:, 0:8], in_values=sc[:], imm_value=-1e9), [bsc], [bsc])
                P.op("dve", lambda e: e.max(out=m8[:, 8:16], in_=sc2[:]), [bsc], [bsc])
                P.ts(selb[:], sc[:], m8[:, 15:16], NEG, ALU.is_lt, ALU.mult, [bsc], [bsc])
                nbk = 2 * (i + 1)
                P.cp(biasf[par][:, 0:nbk, :], selb[:, 0:nbk].unsqueeze(2).broadcast_to([128, nbk, 64]), [bsc], [bbias[par]])
                P.tt(biasv[par][:, i * 128:(i + 1) * 128], biasv[par][:, i * 128:(i + 1) * 128], C("winb", 512, 640), ALU.add,
                     [bbias[par]], [bbias[par]])
                active = [lane_slc(g, i), lane_win(g, i)]
                if os.environ.get("KLANES") == "seq":
                    for gen_ in active:
                        for _ in gen_:
                            pass
                    active = []
                while active:
                    for gen_ in list(active):
                        try:
                            next(gen_)
                        except StopIteration:
                            active.remove(gen_)
                P.cp(obf4[:], oacc3[par][:], boacc[par], [bobf], eng="pool")
                for h in range(4):
                    P.tr(ps_t[:, h, :], obf4[:, h, :], identb[:], [bobf], [bpt], inc=(h == 3))
                P.cp(oast3[:, :, qc], ps_t[:, 0:4, :], [bpt], [boast], eng="act")
            for h in range(4):
                hd = g * 4 + h
                P.dma("sp", oaT[hd * 128:(hd + 1) * 128, :], oast[h][:], reads=[boast])
        P.phase_end()

        P.phase_begin()
        ws = WStream()
        HT = 1024
        oah = arena[:, 0:16384].rearrange("p (k t) -> p k t", t=HT)
        obh = arena[:, 16384:32768].rearrange("p (k t) -> p k t", t=HT)
        mth = arena[:, 32768:49152].rearrange("p (k t) -> p k t", t=HT)
        boah, bobh, bmth = Buf(), Buf(), Buf()
        psy = [P.tile("psy%d" % i, [128, HT], F32, psum=True) for i in range(4)]
        bpsy = [Buf() for _ in range(4)]
        sgt = [P.tile("sgt%d" % i, [128, 2, HT], BF16) for i in range(2)]
        bsgt = [Buf() for _ in range(2)]
        m1 = [P.tile("m1_%d" % i, [128, HT], F32) for i in range(2)]
        bm1 = [Buf() for _ in range(2)]
        xo_ = [P.tile("xold%d" % i, [128, HT], F32) for i in range(2)]
        bxo_ = [Buf() for _ in range(2)]
        for th in range(2):
            t0 = th * HT
            for k4 in range(4):
                P.dma("sp", oah[:, k4 * 4:(k4 + 1) * 4, :],
                      oaT[k4 * 512:(k4 + 1) * 512, t0:t0 + HT].rearrange("(k p) t -> p k t", p=128), writes=[boah])
                P.dma("sp", obh[:, k4 * 4:(k4 + 1) * 4, :],
                      obT[k4 * 512:(k4 + 1) * 512, t0:t0 + HT].rearrange("(k p) t -> p k t", p=128), writes=[bobh])
            fi = 0
            for c4 in range(4):
                wa_, bwa = ws.load(proj_a[l][:, c4 * 512:(c4 + 1) * 512], KC, 512)
                wb_, bwb = ws.load(proj_b[l][:, c4 * 512:(c4 + 1) * 512], KC, 512)
                for m in range(4):
                    f = c4 * 4 + m
                    s = fi % 2
                    fi += 1
                    pa, pbb = 2 * s, 2 * s + 1
                    P.dma("sp", sgt[s][:, 0, :], sgaT[f * 128:(f + 1) * 128, t0:t0 + HT], writes=[bsgt[s]])
                    P.dma("sp", sgt[s][:, 1, :], sgbT[f * 128:(f + 1) * 128, t0:t0 + HT], writes=[bsgt[s]])
                    for (pp, wv, bw, src, bsrc) in ((pa, wa_, bwa, oah, boah), (pbb, wb_, bwb, obh, bobh)):
                        for k in range(KC):
                            for n in range(2):
                                P.mm(psy[pp][:, n * 512:(n + 1) * 512], wv[:, k, m * 128:(m + 1) * 128],
                                     src[:, k, n * 512:(n + 1) * 512], k == 0, k == KC - 1, [bw, bsrc], [bpsy[pp]],
                                     inc=(k == KC - 1 and n == 1))
                    P.tt(m1[s][:], psy[pa][:], sgt[s][:, 0, :], ALU.mult, [bpsy[pa], bsgt[s]], [bm1[s]])
                    P.tt(sgt[s][:, 1, :], psy[pbb][:], sgt[s][:, 1, :], ALU.mult, [bpsy[pbb], bsgt[s]], [bsgt[s]])
                    P.tt(mth[:, f, :], m1[s][:], sgt[s][:, 1, :], ALU.add, [bm1[s], bsgt[s]], [bmth], eng="pool")
            fi = 0
            for c4 in range(4):
                wo_, bwo = ws.load(w_out[l][:, c4 * 512:(c4 + 1) * 512], KC, 512)
                for m in range(4):
                    f = c4 * 4 + m
                    pp = fi % 4
                    s = fi % 2
                    fi += 1
                    P.dma("sp", xo_[s][:], xT[f * 128:(f + 1) * 128, t0:t0 + HT], writes=[bxo_[s]])
                    for k in range(KC):
                        for n in range(2):
                            P.mm(psy[pp][:, n * 512:(n + 1) * 512], wo_[:, k, m * 128:(m + 1) * 128],
                                 mth[:, k, n * 512:(n + 1) * 512], k == 0, k == KC - 1, [bwo, bmth], [bpsy[pp]],
                                 inc=(k == KC - 1 and n == 1))
                    P.stt(xo_[s][:], psy[pp][:], g1c[:, f:f + 1], xo_[s][:], ALU.mult, ALU.add, [bpsy[pp], bxo_[s]], [bxo_[s]])
                    P.dma("sp", xT[f * 128:(f + 1) * 128, t0:t0 + HT], xo_[s][:], reads=[bxo_[s]])
        P.phase_end()

        norm_phase(A2[:, l, :], sh2)

        P.phase_begin()
        ws = WStream()
        psg = [P.tile("psg%d" % i, [128, 1024], F32, psum=True) for i in range(4)]
        bpsg = [Buf() for _ in range(4)]
        gp = [P.tile("gp%d" % i, [128, T + 2], F32) for i in range(2)]
        bgp = [Buf() for _ in range(2)]
        vv = [AR(16384 + i * T, T) for i in range(2)]
        bvv = [Buf() for _ in range(2)]
        tmpg = AR(16384 + 2 * T, T); btmpg = Buf()
        gc = P.tile("gc", [128, T], F32); bgc = Buf()
        hb = [P.tile("hb%d" % i, [128, T], BF16) for i in range(2)]
        bhb = [Buf() for _ in range(2)]
        for i in range(2):
            P.op("pool", lambda e, i=i: e.memset(gp[i][:, 0:2], 0.0), [], [bgp[i]])
        for c4 in range(12):
            wg_, bwg_ = ws.load(ffn_up[l][:, c4 * 512:(c4 + 1) * 512], KC, 512)
            wv_, bwv_ = ws.load(ffn_up[l][:, DFF + c4 * 512:DFF + (c4 + 1) * 512], KC, 512)
            for m in range(4):
                j = c4 * 4 + m
                s = j % 2
                for hf in range(2):
                    for (pp, wv, bw) in ((2 * hf, wg_, bwg_), (2 * hf + 1, wv_, bwv_)):
                        for k in range(KC):
                            for n in range(2):
                                tn = hf * 1024 + n * 512
                                P.mm(psg[pp][:, n * 512:(n + 1) * 512], wv[:, k, m * 128:(m + 1) * 128],
                                     hT[:, k, tn:tn + 512], k == 0, k == KC - 1, [bw, bH[k]], [bpsg[pp]],
                                     inc=(k == KC - 1 and n == 1))
                    P.cp(gp[s][:, 2 + hf * 1024:2 + (hf + 1) * 1024], psg[2 * hf][:], [bpsg[2 * hf]], [bgp[s]], eng="act")
                    P.cp(vv[s][:, hf * 1024:(hf + 1) * 1024], psg[2 * hf + 1][:], [bpsg[2 * hf + 1]], [bvv[s]], eng="act")
                fw_ = CPc(l, "fcw", j * 3, j * 3 + 3)
                P.ts(gc[:], gp[s][:, 2:T + 2], fw_[:, 2:3], CPc(l, "fcb", j, j + 1), ALU.mult, ALU.add, [bgp[s]], [bgc])
                P.stt(gc[:], gp[s][:, 1:T + 1], fw_[:, 1:2], gc[:], ALU.mult, ALU.add, [bgp[s], bgc], [bgc])
                P.stt(gc[:], gp[s][:, 0:T], fw_[:, 0:1], gc[:], ALU.mult, ALU.add, [bgp[s], bgc], [bgc])
                P.gelu(None, gc[:], tmpg, [bgc], btmpg, None)
                P.tt(tmpg, tmpg, gc[:], ALU.mult, [btmpg, bgc], [btmpg])
                P.tt(hb[s][:], tmpg, vv[s], ALU.mult, [btmpg, bvv[s]], [bhb[s]], eng="pool")
                P.dma("sp", hidT[j * 128:(j + 1) * 128, :], hb[s][:], reads=[bhb[s]])
        P.phase_end()

        P.phase_begin()
        ws = WStream()
        hdh = arena[:, 0:49152].rearrange("p (k t) -> p k t", t=HT)
        bhd = [Buf() for _ in range(6)]
        psd = [P.tile("psd%d" % i, [128, HT], F32, psum=True) for i in range(4)]
        bpsd = [Buf() for _ in range(4)]
        xo2 = [P.tile("xold2_%d" % i, [128, HT], F32) for i in range(2)]
        bxo2 = [Buf() for _ in range(2)]
        for th in range(2):
            t0 = th * HT
            for k8 in range(6):
                P.dma("sp", hdh[:, k8 * 8:(k8 + 1) * 8, :],
                      hidT[k8 * 1024:(k8 + 1) * 1024, t0:t0 + HT].rearrange("(k p) t -> p k t", p=128), writes=[bhd[k8]])
            fi = 0
            for c2 in range(16):
                wd_, bwd = ws.load(ffn_down[l][:, c2 * 128:(c2 + 1) * 128], 48, 128)
                for m in range(1):
                    f = c2
                    pp = fi % 4
                    s = fi % 2
                    fi += 1
                    P.dma("sp", xo2[s][:], xT[f * 128:(f + 1) * 128, t0:t0 + HT], writes=[bxo2[s]])
                    for k in range(48):
                        for n in range(2):
                            P.mm(psd[pp][:, n * 512:(n + 1) * 512], wd_[:, k, m * 128:(m + 1) * 128],
                                 hdh[:, k, n * 512:(n + 1) * 512], k == 0, k == 47, [bwd, bhd[k // 8]], [bpsd[pp]],
                                 inc=(k == 47 and n == 1))
                    P.stt(xo2[s][:], psd[pp][:], g2c[:, f:f + 1], xo2[s][:], ALU.mult, ALU.add, [bpsd[pp], bxo2[s]], [bxo2[s]])
                    P.dma("sp", xT[f * 128:(f + 1) * 128, t0:t0 + HT], xo2[s][:], reads=[bxo2[s]])
        P.phase_end()

    fg = C("fing")
    norm_phase(fg, None, final=True)
    P.finish()
    return nc, P


def _consts():
    q = np.arange(128)
    cst = np.zeros((128, NCS), np.float32)

    def put(name, arr):
        o, w = CS[name]
        cst[:, o:o + w] = arr.reshape(128, w)
    t = (np.arange(16)[None, :] * 128 + q[:, None])
    put("rowv", (t >= 31).astype(np.float32))
    n = np.arange(128)
    cm = (16 * n[None, None, :] + 31 <= t[:, :, None]) & (n[None, None, :] < 127)
    put("cmpb", np.where(cm, 0.0, NEG).astype(np.float32))
    wb = np.zeros((128, 640), np.float32)
    kk = np.arange(128)
    wb[:, 0:128] = np.where(kk[None, :] > q[:, None], 0.0, NEG)
    wb[:, 512:640] = np.where(kk[None, :] <= q[:, None], 0.0, NEG)
    put("winb", wb)
    blk = np.arange(32)[None, None, :]
    cur = (t // 64)[:, :, None]
    valid = blk <= cur
    forced = (blk == 0) | (valid & (blk > cur - 2))
    put("validm", valid.astype(np.float32))
    put("addc", np.where(forced, 1e4, np.where(valid, 0.0, -1.0)).astype(np.float32))
    cs = np.arange(128) * 16
    ss = np.arange(32) * 64
    ov = ((cs[:, None] < ss[None, :] + 64) & (cs[:, None] + 32 > ss[None, :])).astype(np.float32)
    ov[127, :] = 0.0
    put("ovl", ov)
    put("ident", np.eye(128, dtype=np.float32))
    return cst


def _col(v, n):
    return np.swapaxes(v.reshape(v.shape[:-1] + (n, 128)), -1, -2)


_CACHE = {}


def kernel(x, c, ada_w, ada_b, norm1_g, w_in, cmp_pos_k, cmp_w1_k, cmp_w2_k, cmp_pos_v, cmp_w1_v, cmp_w2_v,
           lru_conv_w, lru_conv_b, lru_wa, lru_ba, lru_wi, lru_bi, lru_lambda, proj_a, proj_b, w_out, norm2_g,
           ffn_up, ffn_conv_w, ffn_conv_b, ffn_down, final_g, _cores=None, _dbg=False, _stop=None):
    f = lambda a: np.ascontiguousarray(np.asarray(a, dtype=np.float32))
    x = f(x)
    L = int(np.asarray(w_in).shape[0])
    B = x.shape[0]
    cores = list(range(B)) if _cores is None else list(_cores)
    colp = np.zeros((L, 128, NCP), np.float32)

    def putc(name, arr):
        o, w = CP[name]
        colp[:, :, o:o + w] = arr.reshape(L, 128, w)
    putc("adab", _col(f(ada_b), 96))
    putc("n1g", _col(f(norm1_g), 16))
    putc("n2g", _col(f(norm2_g), 16))
    putc("lcw", np.transpose(f(lru_conv_w).reshape(L, 4, 16, 128), (0, 3, 2, 1)))
    putc("lcb", _col(f(lru_conv_b), 16))
    putc("lba", _col(f(lru_ba), 16))
    putc("lbi", _col(f(lru_bi), 16))
    putc("llam", _col(f(lru_lambda), 16))
    putc("fcw", np.transpose(f(ffn_conv_w).reshape(L, 3, 48, 128), (0, 3, 2, 1)))
    putc("fcb", _col(f(ffn_conv_b), 48))
    putc("posk", np.transpose(f(cmp_pos_k), (0, 2, 1)))
    putc("posv", np.transpose(f(cmp_pos_v), (0, 2, 1)))
    cst0 = _consts()
    fg = _col(f(final_g), 16)
    shared = {"colp": colp, "ada_w": f(ada_w), "w_in": f(w_in), "cmp_w1_k": f(cmp_w1_k), "cmp_w2_k": f(cmp_w2_k),
              "cmp_w1_v": f(cmp_w1_v), "cmp_w2_v": f(cmp_w2_v), "lru_wa": f(lru_wa), "lru_wi": f(lru_wi),
              "proj_a": f(proj_a), "proj_b": f(proj_b), "w_out": f(w_out), "ffn_up": f(ffn_up), "ffn_down": f(ffn_down)}
    in_maps = []
    cc = f(c)
    for b in cores:
        cst = cst0.copy()
        o, w = CS["cT"]
        cst[:, o:o + w] = _col(cc[b], 16)
        o, w = CS["fing"]
        cst[:, o:o + w] = fg
        m = dict(shared)
        m["x"] = x[b]
        m["cst"] = cst
        in_maps.append(m)
    key = (L, _dbg, _stop)
    if key not in _CACHE:
        _CACHE[key] = build(L, _dbg, _stop)[0]
    nc = _CACHE[key]
    res = run_bass_kernel_spmd(nc, in_maps, core_ids=list(range(len(cores))))
    if _dbg:
        return res.results
    return np.stack([np.asarray(r["out"], dtype=np.float32) for r in res.results], axis=0)
```

```python
import os
import numpy as np
from contextlib import ExitStack
import concourse.bass as bass
import concourse.mybir as mybir
from concourse.bass_utils import run_bass_kernel_spmd

F32 = mybir.dt.float32
BF16 = mybir.dt.bfloat16
AF = mybir.ActivationFunctionType
ALU = mybir.AluOpType
AX = mybir.AxisListType

T = 2048
D = 2048
KC = 16
DFF = 6144
INW = 13360
NEG = -30000.0

CP = {}
_o = 0
for _n, _w in (("adab", 96), ("n1g", 16), ("n2g", 16), ("lcw", 64), ("lcb", 16), ("lba", 16), ("lbi", 16),
               ("llam", 16), ("fcw", 144), ("fcb", 48), ("posk", 32), ("posv", 32)):
    CP[_n] = (_o, _w)
    _o += _w
NCP = _o
CS = {}
_o = 0
for _n, _w in (("cT", 16), ("fing", 16), ("rowv", 16), ("cmpb", 2048), ("winb", 640), ("validm", 512),
               ("addc", 512), ("ovl", 32), ("ident", 128)):
    CS[_n] = (_o, _w)
    _o += _w
NCS = _o


class Buf:
    __slots__ = ("writer", "readers")

    def __init__(self):
        self.writer = None
        self.readers = {}


class StopBuild(Exception):
    pass


class Prog:
    stop_at = None
    ENGS = ("pe", "act", "dve", "pool", "sp")
    NSLOT = 8

    def __init__(self, nc):
        self.nc = nc
        self.es = ExitStack()
        self.q = {e: [] for e in self.ENGS}
        self.cnt = {e: 0 for e in self.ENGS}
        self.seen = {e: {} for e in self.ENGS}
        self.sems = {}
        for e in ("pe", "act", "dve", "pool"):
            self.sems["c_" + e] = self.es.enter_context(nc.semaphore("c_" + e))
        self.dma_i = {}
        self.dma_tgt = {}
        for e in ("sp", "pool", "act"):
            self.dma_i[e] = 0
            for s in range(self.NSLOT):
                k = "d_%s_%d" % (e, s)
                self.sems[k] = self.es.enter_context(nc.semaphore(k))
                self.dma_tgt[k] = 0
        self.out_toks = []
        self.n_ins = 0
        self.n_t = 0
        self.ps = None

    skip_set = ()
    skip = False

    def phase_begin(self):
        self.ps = ExitStack()
        self.skip = getattr(self, "n_phase", 0) in self.skip_set

    def tile(self, name, shape, dt, psum=False):
        st = self.ps if self.ps is not None else self.es
        self.n_t += 1
        name = "%s_t%d" % (name, self.n_t)
        if psum:
            return st.enter_context(self.nc.psum_tensor(name, list(shape), dt))
        return st.enter_context(self.nc.sbuf_tensor(name, list(shape), dt))

    def barrier(self):
        for e in self.ENGS:
            seen = self.seen[e]
            for f in ("pe", "act", "dve", "pool"):
                k = "c_" + f
                if f != e and self.cnt[f] > 0 and seen.get(k, 0) < self.cnt[f]:
                    self.q[e].append(("wait", k, self.cnt[f]))
                    seen[k] = self.cnt[f]
            for k, v in self.dma_tgt.items():
                if v > 0 and seen.get(k, 0) < v:
                    self.q[e].append(("wait", k, v))
                    seen[k] = v

    def phase_end(self):
        self.skip = False
        self.barrier()
        self.flush()
        self.ps.close()
        self.ps = None
        self.n_phase = getattr(self, "n_phase", 0) + 1
        if self.stop_at is not None and self.n_phase >= self.stop_at:
            self.finish()
            raise StopBuild(self)

    def flush(self):
        nc = self.nc
        handles = {"pe": "tensor", "act": "scalar", "dve": "vector", "pool": "gpsimd", "sp": "sync"}
        with nc.Block() as block:
            for e in self.ENGS:
                q = self.q[e]
                sems = self.sems

                def body(eng, q=q, sems=sems):
                    for it in q:
                        if it[0] == "wait":
                            eng.wait_ge(sems[it[1]], it[2])
                        else:
                            ins = it[1](eng)
                            if it[2] is not None:
                                ins.then_inc(sems[it[2]], it[3])
                getattr(block, handles[e])(body)
        tot = getattr(self, "tot", {e: 0 for e in self.ENGS})
        for e in self.ENGS:
            tot[e] += len(self.q[e])
        self.tot = tot
        self.q = {e: [] for e in self.ENGS}

    def _deps(self, eng, reads, writes):
        deps = []
        for b in reads:
            if b.writer is not None:
                deps.append(b.writer)
        for b in writes:
            deps.extend(b.readers.values())
            if b.writer is not None:
                deps.append(b.writer)
        seen = self.seen[eng]
        for (k, v, src) in deps:
            if src == eng and eng == "pe":
                continue
            if seen.get(k, 0) >= v:
                continue
            self.q[eng].append(("wait", k, v))
            seen[k] = v

    def _mark(self, tok, reads, writes):
        for b in reads:
            b.readers[tok[0]] = tok
        for b in writes:
            b.writer = tok
            b.readers = {}

    def op(self, eng, fn, reads=(), writes=(), inc=True):
        if self.skip:
            return None
        self._deps(eng, reads, writes)
        k = "c_" + eng
        if inc:
            self.cnt[eng] += 1
            tok = (k, self.cnt[eng], eng)
        else:
            tok = (k, self.cnt[eng] + 1, eng)
        self.q[eng].append(("ins", fn, k if inc else None, 1))
        self._mark(tok, reads, writes)
        self.n_ins += 1
        return tok

    def dma(self, qe, out, in_, reads=(), writes=(), is_output=False, **kw):
        if self.skip:
            return None
        i = self.dma_i[qe]
        self.dma_i[qe] = i + 1
        k = "d_%s_%d" % (qe, i % self.NSLOT)
        prev = self.dma_tgt[k]
        seen = self.seen[qe]
        if prev > 0 and seen.get(k, 0) < prev:
            self.q[qe].append(("wait", k, prev))
            seen[k] = prev
        self._deps(qe, reads, writes)
        tgt = prev + 16
        self.dma_tgt[k] = tgt
        tok = (k, tgt, "dma")

        def fn(e, out=out, in_=in_, kw=kw):
            return e.dma_start(out=out, in_=in_, **kw)
        self.q[qe].append(("ins", fn, k, 16))
        self._mark(tok, reads, writes)
        if is_output:
            self.out_toks.append(tok)
        self.n_ins += 1
        return tok

    def finish(self):
        for (k, v, _) in self.out_toks:
            if self.seen["sp"].get(k, 0) < v:
                self.q["sp"].append(("wait", k, v))
                self.seen["sp"][k] = v
        self.flush()
        self.es.close()

    def mm(self, out, lhsT, rhs, start, stop, reads, writes, inc=None):
        self.op("pe", lambda e: e.matmul(out, lhsT=lhsT, rhs=rhs, start=start, stop=stop), reads, writes,
                inc=stop if inc is None else inc)

    def tr(self, out, in_, ident, reads, writes, inc=True):
        self.op("pe", lambda e: e.transpose(out=out, in_=in_, identity=ident), reads, writes, inc=inc)

    def act(self, out, in_, func, reads, writes, **kw):
        self.op("act", lambda e: e.activation(out=out, in_=in_, func=func, **kw), reads, writes)

    def tt(self, out, a, b, op, reads, writes, eng="dve"):
        self.op(eng, lambda e: e.tensor_tensor(out=out, in0=a, in1=b, op=op), reads, writes)

    def ts(self, out, a, s1, s2, op0, op1, reads, writes, eng="dve"):
        self.op(eng, lambda e: e.tensor_scalar(out=out, in0=a, scalar1=s1, scalar2=s2, op0=op0, op1=op1), reads, writes)

    def stt(self, out, a, s, b, op0, op1, reads, writes):
        self.op("dve", lambda e: e.scalar_tensor_tensor(out=out, in0=a, scalar=s, in1=b, op0=op0, op1=op1), reads, writes)

    def gelu(self, out, x, tm, rx, btm, wout, eng2="pool"):
        self.act(tm, x, AF.Square, rx, [btm])
        self.ts(tm, tm, 0.044715, 1.0, ALU.mult, ALU.add, [btm], [btm], eng=eng2)
        self.tt(tm, tm, x, ALU.mult, [btm] + rx, [btm])
        self.act(tm, tm, AF.Sigmoid, [btm], [btm], scale=1.5957691216057308)
        if out is not None:
            self.tt(out, tm, x, ALU.mult, [btm] + rx, wout)

    def cp(self, out, in_, reads, writes, eng="dve"):
        if eng == "act":
            self.act(out, in_, AF.Copy, reads, writes)
        else:
            self.op(eng, lambda e: e.tensor_copy(out=out, in_=in_), reads, writes)


def build(L, dbg=False, stop=None):
    nc = bass.Bass("TRN2", target_bir_lowering=False)
    Prog.stop_at = stop
    import os
    Prog.skip_set = tuple(int(v) for v in os.environ.get("KSKIP", "").split(",") if v)
    try:
        return _build(nc, L, dbg)
    except StopBuild as e:
        P = e.args[0] if e.args else None
        return nc, P


def _build(nc, L, dbg):

    def din(name, shape):
        return nc.dram_tensor(name, list(shape), F32, kind="ExternalInput").ap()

    x_in = din("x", [T, D])
    blk_in = din("blkind", [128, T])
    cst_in = din("cst", [128, NCS])
    colp_in = din("colp", [L, 128, NCP])
    ada_w = din("ada_w", [L, D, 6 * D])
    w_in = din("w_in", [L, D, INW])
    w1k = din("cmp_w1_k", [L, 32, 128, 128])
    w2k = din("cmp_w2_k", [L, 128, 128])
    w1v = din("cmp_w1_v", [L, 32, 128, 128])
    w2v = din("cmp_w2_v", [L, 128, 128])
    lru_wa = din("lru_wa", [L, 16, 128, 128])
    lru_wi = din("lru_wi", [L, 16, 128, 128])
    proj_a = din("proj_a", [L, D, D])
    proj_b = din("proj_b", [L, D, D])
    w_out = din("w_out", [L, D, D])
    ffn_up = din("ffn_up", [L, D, 2 * DFF])
    ffn_down = din("ffn_down", [L, DFF, D])
    out = nc.dram_tensor("out", [T, D], F32, kind="ExternalOutput").ap()
    sk = "ExternalOutput" if dbg else "Internal"

    def dsc(name, shape, dt):
        return nc.dram_tensor(name, list(shape), dt, kind=sk).ap()

    xT = dsc("s_xT", [D, T], F32)
    qT = dsc("s_qT", [D, T], BF16)
    kcT = dsc("s_kcT", [512, T], BF16)
    vcT = dsc("s_vcT", [512, T], BF16)
    ksT = dsc("s_ksT", [512, T], BF16)
    kwT = dsc("s_kwT", [512, T], BF16)
    vsd = dsc("s_vs", [T, 512], BF16)
    vwd = dsc("s_vw", [T, 512], BF16)
    gtd = dsc("s_gates", [T, 48], F32)
    lxT = dsc("s_lxT", [D, T], F32)
    glyT = dsc("s_glyT", [D, T], BF16)
    sgaT = dsc("s_sgaT", [D, T], BF16)
    sgbT = dsc("s_sgbT", [D, T], BF16)
    obT = dsc("s_obT", [D, T], BF16)
    oaT = dsc("s_oaT", [D, T], BF16)
    hidT = dsc("s_hidT", [DFF, T], BF16)

    P = Prog(nc)
    cst = P.tile("cst", [128, NCS], F32)
    colp = P.tile("colp", [128, L, NCP], F32)
    modc = P.tile("modc", [128, L, 96], F32)
    A1 = P.tile("A1", [128, L, 16], F32)
    A2 = P.tile("A2", [128, L, 16], F32)
    clam = P.tile("clam", [128, L, 16], F32)
    clam2 = P.tile("clam2", [128, L, 16], F32)
    cact = P.tile("cact", [128, 16], F32)
    identb = P.tile("identb", [128, 128], BF16)
    ovlb = P.tile("ovlb", [128, 32], BF16)
    onesf = P.tile("onesf", [128, 128], F32)
    arena = P.tile("arena", [128, 49152], BF16)
    hT = arena[:, 0:32768].rearrange("p (k t) -> p k t", t=T)
    arena_f = arena[:].bitcast(F32)

    def AR(o, n):
        return arena_f[:, o:o + n]

    def ARB(o, n):
        return arena[:, o:o + n]
    bH = [Buf() for _ in range(KC)]

    def C(name, a=0, b=None):
        o, w = CS[name]
        return cst[:, o + a:o + (w if b is None else b)]

    def CPc(l, name, a=0, b=None):
        o, w = CP[name]
        return colp[:, l, o + a:o + (w if b is None else b)]

    ident_f = C("ident")

    P.phase_begin()
    bC = Buf()
    P.dma("sp", cst[:], cst_in, writes=[bC])
    P.dma("sp", colp[:], colp_in.rearrange("l p n -> p l n"), writes=[bC])
    P.cp(identb[:], C("ident"), [bC], [bC])
    P.cp(ovlb[:], C("ovl"), [bC], [bC])
    P.op("dve", lambda e: e.memset(onesf[:], 1.0), [], [bC])
    e1 = P.tile("e1", [128, 16], F32)
    P.act(e1[:], C("cT"), AF.Exp, [bC], [bC], scale=-1.0)
    P.ts(e1[:], e1[:], 1.0, None, ALU.add, ALU.bypass, [bC], [bC])
    P.op("dve", lambda e: e.reciprocal(out=e1[:], in_=e1[:]), [bC], [bC])
    P.tt(cact[:], e1[:], C("cT"), ALU.mult, [bC], [bC])
    for l in range(L):
        P.act(clam[:, l, :], CPc(l, "llam"), AF.Exp, [bC], [bC], scale=-1.0)
    for l in range(L):
        P.act(clam[:, l, :], clam[:, l, :], AF.Ln, [bC], [bC], bias=1.0)
    for l in range(L):
        P.ts(clam2[:, l, :], clam[:, l, :], -16.0, None, ALU.mult, ALU.bypass, [bC], [bC])
        P.ts(clam[:, l, :], clam[:, l, :], -8.0, None, ALU.mult, ALU.bypass, [bC], [bC])
    NST = 2
    ast = [P.tile("ast%d" % i, [128, 16, 512], F32) for i in range(NST)]
    bast = [Buf() for _ in range(NST)]
    psm = P.tile("psm", [128, 96], F32, psum=True)
    bpsm = Buf()
    ci = 0
    for l in range(L):
        awv = ada_w[l].rearrange("(k p) m -> p k m", p=128)
        for ch in range(24):
            s = ci % NST
            ci += 1
            P.dma("sp", ast[s][:], awv[:, :, ch * 512:(ch + 1) * 512], writes=[bast[s]])
            for j in range(4):
                col = ch * 4 + j
                for k in range(KC):
                    P.mm(psm[:, col:col + 1], ast[s][:, k, j * 128:(j + 1) * 128], cact[:, k:k + 1],
                         k == 0, k == KC - 1, [bast[s], bC], [bpsm])
        P.tt(modc[:, l, :], psm[:], CPc(l, "adab"), ALU.add, [bpsm, bC], [bC, bpsm])
        P.ts(A1[:, l, :], modc[:, l, 16:32], 1.0, None, ALU.add, ALU.bypass, [bC], [bC])
        P.tt(A1[:, l, :], A1[:, l, :], CPc(l, "n1g"), ALU.mult, [bC], [bC])
        P.ts(A2[:, l, :], modc[:, l, 64:80], 1.0, None, ALU.add, ALU.bypass, [bC], [bC])
        P.tt(A2[:, l, :], A2[:, l, :], CPc(l, "n2g"), ALU.mult, [bC], [bC])
    P.phase_end()

    P.phase_begin()
    xt = [P.tile("xt%d" % i, [128, D], F32) for i in range(2)]
    bxt = [Buf() for _ in range(2)]
    xo = [P.tile("xo%d" % i, [128, 16, 128], F32) for i in range(2)]
    bxo = [Buf() for _ in range(2)]
    pst = [P.tile("pst%d" % i, [128, 4, 128], F32, psum=True) for i in range(4)]
    bpst = [Buf() for _ in range(4)]
    pi = 0
    xTv = xT.rearrange("(k p) t -> p k t", p=128)
    for tt_ in range(16):
        s = tt_ % 2
        P.dma("sp", xt[s][:], x_in[tt_ * 128:(tt_ + 1) * 128, :], writes=[bxt[s]])
        for g4 in range(4):
            pp = pi % 4
            pi += 1
            for j in range(4):
                fc = g4 * 4 + j
                P.tr(pst[pp][:, j, :], xt[s][:, fc * 128:(fc + 1) * 128], ident_f, [bxt[s]], [bpst[pp]], inc=(j == 3))
            P.cp(xo[s][:, g4 * 4:(g4 + 1) * 4, :], pst[pp][:], [bpst[pp]], [bxo[s]], eng=("act" if g4 % 2 else "dve"))
        P.dma("sp", xTv[:, :, tt_ * 128:(tt_ + 1) * 128], xo[s][:], reads=[bxo[s]])
    P.phase_end()

    def norm_phase(Acol, Bcol, final=False):
        P.phase_begin()
        NW = 256
        xc = [P.tile("xc%d" % i, [128, 16, NW], F32) for i in range(2)]
        bxc = [Buf() for _ in range(2)]
        sq = [P.tile("sq%d" % i, [128, NW], F32) for i in range(3)]
        bsq = [Buf() for _ in range(3)]
        pss = P.tile("pss", [128, 512], F32, psum=True)
        bpss = Buf()
        rinv = P.tile("rinv", [128, NW], F32)
        brinv = Buf()
        t1 = [P.tile("t1%d" % i, [128, NW], F32) for i in range(3)]
        bt1 = [Buf() for _ in range(3)]
        if final:
            ot = [P.tile("ot%d" % i, [128, D], F32) for i in range(4)]
            bot = [Buf() for _ in range(4)]
            psf = [P.tile("psf%d" % i, [128, 4, 128], F32, psum=True) for i in range(2)]
            bpsf = [Buf() for _ in range(2)]
        for n in range(T // NW):
            s = n % 2
            P.dma("sp", xc[s][:], xTv[:, :, n * NW:(n + 1) * NW], writes=[bxc[s]])
            for fc in range(KC):
                q3 = fc % 3
                P.act(sq[q3][:], xc[s][:, fc, :], AF.Square, [bxc[s]], [bsq[q3]])
                P.mm(pss[:, 0:NW], onesf[:], sq[q3][:], fc == 0, fc == KC - 1, [bsq[q3]], [bpss], inc=True)
            P.act(rinv[:], pss[:, 0:NW], AF.Sqrt, [bpss], [brinv], bias=1e-6, scale=1.0 / D)
            P.op("dve", lambda e: e.reciprocal(out=rinv[:], in_=rinv[:]), [brinv], [brinv])
            for fc in range(KC):
                q3 = fc % 3
                P.tt(t1[q3][:], xc[s][:, fc, :], rinv[:], ALU.mult, [bxc[s], brinv], [bt1[q3]])
                if not final:
                    P.act(hT[:, fc, n * NW:(n + 1) * NW], t1[q3][:], AF.Identity, [bt1[q3]], [bH[fc]],
                          scale=Acol[:, fc:fc + 1], bias=Bcol[:, fc:fc + 1])
                else:
                    P.act(t1[q3][:], t1[q3][:], AF.Identity, [bt1[q3]], [bt1[q3]], scale=Acol[:, fc:fc + 1])
                    pf = fc % 2
                    for j in range(2):
                        P.tr(psf[pf][:, j, :], t1[q3][:, j * 128:(j + 1) * 128], ident_f, [bt1[q3]], [bpsf[pf]], inc=(j == 1))
                    for j in range(2):
                        oj = (n % 2) * 2 + j
                        P.cp(ot[oj][:, fc * 128:(fc + 1) * 128], psf[pf][:, j, :], [bpsf[pf]], [bot[oj]],
                             eng=("act" if pf else "dve"))
            if final:
                for j in range(2):
                    oj = (n % 2) * 2 + j
                    r0 = n * NW + j * 128
                    P.dma("sp", out[r0:r0 + 128, :], ot[oj][:], reads=[bot[oj]], is_output=True)
        P.phase_end()

    class WStream:
        def __init__(self, nb=3, width=8192):
            self.t = [P.tile("wb%d" % i, [128, width], BF16) for i in range(nb)]
            self.b = [Buf() for _ in range(nb)]
            self.i = 0
            self.nb = nb

        def load(self, wsrc, kc, ncols):
            s = self.i % self.nb
            self.i += 1
            v = self.t[s][:, 0:kc * ncols].rearrange("p (k m) -> p k m", m=ncols)
            P.dma("pool", v, wsrc.rearrange("(k p) m -> p k m", p=128), writes=[self.b[s]])
            return v, self.b[s]

    for l in range(L):
        sh1, g1c = modc[:, l, 0:16], modc[:, l, 32:48]
        sh2, g2c = modc[:, l, 48:64], modc[:, l, 80:96]
        norm_phase(A1[:, l, :], sh1)

        P.phase_begin()
        ws = WStream()
        psz = [P.tile("psz%d" % i, [128, T], F32, psum=True) for i in range(2)]
        bpsz = [Buf() for _ in range(2)]
        ob16 = [P.tile("ob16_%d" % i, [128, T], BF16) for i in range(3)]
        bob16 = [Buf() for _ in range(3)]
        of32 = [P.tile("of32_%d" % i, [128, T], F32) for i in range(2)]
        bof32 = [Buf() for _ in range(2)]
        cnt = {"p": 0, "o": 0, "f": 0}

        def fm_chunk(col0, kind, dst, row0):
            wv, bw = ws.load(w_in[l][:, col0:col0 + 512], KC, 512)
            for m in range(4):
                pp = cnt["p"] % 2
                cnt["p"] += 1
                for k in range(KC):
                    for n in range(4):
                        P.mm(psz[pp][:, n * 512:(n + 1) * 512], wv[:, k, m * 128:(m + 1) * 128],
                             hT[:, k, n * 512:(n + 1) * 512], k == 0, k == KC - 1, [bw, bH[k]], [bpsz[pp]],
                             inc=(k == KC - 1 and n == 3))
                r = row0 + m * 128
                if kind == "lx":
                    o = cnt["f"] % 2
                    cnt["f"] += 1
                    P.cp(of32[o][:], psz[pp][:], [bpsz[pp]], [bof32[o]], eng="dve")
                    P.dma("sp", dst[r:r + 128, :], of32[o][:], reads=[bof32[o]])
                else:
                    o = cnt["o"] % 3
                    cnt["o"] += 1
                    if kind == "q":
                        P.act(ob16[o][:], psz[pp][:], AF.Copy, [bpsz[pp]], [bob16[o]], scale=float(128.0 ** -0.5))
                    elif kind == "copy":
                        P.cp(ob16[o][:], psz[pp][:], [bpsz[pp]], [bob16[o]], eng="dve")
                    elif kind == "gelu":
                        o2 = cnt["f"] % 2
                        cnt["f"] += 1
                        P.gelu(ob16[o][:], psz[pp][:], of32[o2][:], [bpsz[pp]], bof32[o2], [bob16[o]])
                    elif kind == "sig":
                        P.act(ob16[o][:], psz[pp][:], AF.Sigmoid, [bpsz[pp]], [bob16[o]])
                    P.dma("sp", dst[r:r + 128, :], ob16[o][:], reads=[bob16[o]])

        def tm_chunk(col0, ncols, dst, f32out):
            wv, bw = ws.load(w_in[l][:, col0:col0 + ncols], KC, ncols)
            for tt_ in range(16):
                pp = cnt["p"] % 2
                cnt["p"] += 1
                for k in range(KC):
                    P.mm(psz[pp][:, 0:ncols], hT[:, k, tt_ * 128:(tt_ + 1) * 128], wv[:, k, :],
                         k == 0, k == KC - 1, [bw, bH[k]], [bpsz[pp]])
                if f32out:
                    o = cnt["f"] % 2
                    cnt["f"] += 1
                    P.cp(of32[o][:, 0:ncols], psz[pp][:, 0:ncols], [bpsz[pp]], [bof32[o]], eng="dve")
                    P.dma("sp", dst[tt_ * 128:(tt_ + 1) * 128, :], of32[o][:, 0:ncols], reads=[bof32[o]])
                else:
                    o = cnt["o"] % 3
                    cnt["o"] += 1
                    P.cp(ob16[o][:, 0:ncols], psz[pp][:, 0:ncols], [bpsz[pp]], [bob16[o]], eng="act")
                    P.dma("sp", dst[tt_ * 128:(tt_ + 1) * 128, :], ob16[o][:, 0:ncols], reads=[bob16[o]])

        for i in range(4):
            fm_chunk(i * 512, "q", qT, i * 512)
        fm_chunk(2048, "copy", kcT, 0)
        fm_chunk(2560, "copy", vcT, 0)
        fm_chunk(3072, "copy", ksT, 0)
        tm_chunk(3584, 512, vsd, False)
        fm_chunk(4096, "copy", kwT, 0)
        tm_chunk(4608, 512, vwd, False)
        tm_chunk(5120, 48, gtd, True)
        for i in range(4):
            fm_chunk(5168 + i * 512, "lx", lxT, i * 512)
        for i in range(4):
            fm_chunk(7216 + i * 512, "gelu", glyT, i * 512)
        for i in range(4):
            fm_chunk(9264 + i * 512, "sig", sgaT, i * 512)
        for i in range(4):
            fm_chunk(11312 + i * 512, "sig", sgbT, i * 512)
        P.phase_end()

        P.phase_begin()
        wab = P.tile("wab", [128, 16, 128], BF16)
        wib = P.tile("wib", [128, 16, 128], BF16)
        bwg = Buf()
        P.dma("pool", wab[:], lru_wa[l].rearrange("n d e -> d n e"), writes=[bwg])
        P.dma("pool", wib[:], lru_wi[l].rearrange("n d e -> d n e"), writes=[bwg])
        xp = [P.tile("xp%d" % i, [128, T + 3], F32) for i in range(2)]
        bxp = [Buf() for _ in range(2)]
        gly = [P.tile("gly%d" % i, [128, T], BF16) for i in range(2)]
        bgly = [Buf() for _ in range(2)]
        for i in range(2):
            P.op("pool", lambda e, i=i: e.memset(xp[i][:, 0:3], 0.0), [], [bxp[i]])
        U = [AR(s_ * 10240 + 0, T) for s_ in range(2)]; bU = [Buf(), Buf()]
        RR = [AR(s_ * 10240 + 2048, T) for s_ in range(2)]; bRR = [Buf(), Buf()]
        II = [AR(s_ * 10240 + 4096, T) for s_ in range(2)]; bII = [Buf(), Buf()]
        AA = [AR(s_ * 10240 + 6144, T) for s_ in range(2)]; bAA = [Buf(), Buf()]
        MM = [AR(s_ * 10240 + 8192, T) for s_ in range(2)]; bMM = [Buf(), Buf()]
        ub = [P.tile("ub%d" % i, [128, T], BF16) for i in range(2)]
        bub = [Buf(), Buf()]
        obt = [P.tile("obt%d" % i, [128, T], BF16) for i in range(2)]
        bobt = [Buf() for _ in range(2)]
        psr = P.tile("psr", [128, T], F32, psum=True); bpsr = Buf()
        psi = P.tile("psi", [128, T], F32, psum=True); bpsi = Buf()

        def lru_a(ct):
            s = ct % 2
            u, rr, ii, aa, mmx = U[s], RR[s], II[s], AA[s], MM[s]
            P.dma("sp", xp[s][:, 3:T + 3], lxT[ct * 128:(ct + 1) * 128, :], writes=[bxp[s]])
            P.dma("sp", gly[s][:], glyT[ct * 128:(ct + 1) * 128, :], writes=[bgly[s]])
            lcw = CPc(l, "lcw", ct * 4, ct * 4 + 4)
            P.ts(u, xp[s][:, 3:T + 3], lcw[:, 3:4], CPc(l, "lcb", ct, ct + 1), ALU.mult, ALU.add, [bxp[s]], [bU[s]])
            for j in (2, 1, 0):
                P.stt(u, xp[s][:, j:j + T], lcw[:, j:j + 1], u, ALU.mult, ALU.add, [bxp[s], bU[s]], [bU[s]])
            P.cp(ub[s][:], u, [bU[s]], [bub[s]], eng="pool")
            for n in range(4):
                P.mm(psr[:, n * 512:(n + 1) * 512], wab[:, ct, :], ub[s][:, n * 512:(n + 1) * 512], True, True,
                     [bwg, bub[s]], [bpsr], inc=(n == 3))
            for n in range(4):
                P.mm(psi[:, n * 512:(n + 1) * 512], wib[:, ct, :], ub[s][:, n * 512:(n + 1) * 512], True, True,
                     [bwg, bub[s]], [bpsi], inc=(n == 3))
            P.act(rr, psr[:], AF.Sigmoid, [bpsr], [bRR[s]], bias=CPc(l, "lba", ct, ct + 1))
            P.act(ii, psi[:], AF.Sigmoid, [bpsi], [bII[s]], bias=CPc(l, "lbi", ct, ct + 1))
            P.act(aa, rr, AF.Exp, [bRR[s]], [bAA[s]], scale=clam[:, l, ct:ct + 1])
            P.act(mmx, rr, AF.Exp, [bRR[s]], [bMM[s]], scale=clam2[:, l, ct:ct + 1])
            P.ts(mmx, mmx, -1.0, 1.0, ALU.mult, ALU.add, [bMM[s]], [bMM[s]], eng="pool")
            P.ts(mmx, mmx, 1e-20, None, ALU.max, ALU.bypass, [bMM[s]], [bMM[s]], eng="pool")
            P.act(mmx, mmx, AF.Sqrt, [bMM[s]], [bMM[s]])

        def lru_b(ct):
            s = ct % 2
            u, rr, ii, aa, mmx = U[s], RR[s], II[s], AA[s], MM[s]
            P.tt(ii, ii, u, ALU.mult, [bII[s], bU[s]], [bII[s]], eng="pool")
            P.tt(ii, ii, mmx, ALU.mult, [bII[s], bMM[s]], [bII[s]], eng="pool")
            P.op("dve", lambda e: e.tensor_tensor_scan(out=rr, data0=aa, data1=ii, initial=0.0,
                                                       op0=ALU.mult, op1=ALU.add), [bAA[s], bII[s], bRR[s]], [bRR[s]])
            P.tt(obt[s][:], rr, gly[s][:], ALU.mult, [bRR[s], bgly[s]], [bobt[s]])
            P.dma("sp", obT[ct * 128:(ct + 1) * 128, :], obt[s][:], reads=[bobt[s]])

        lru_a(0)
        for ct in range(16):
            if ct + 1 < 16:
                lru_a(ct + 1)
            lru_b(ct)
        P.phase_end()

        P.phase_begin()
        gts = P.tile("gts", [128, 16, 48], F32); bg = Buf()
        P.dma("sp", gts[:], gtd.rearrange("(tt p) c -> p tt c", p=128), writes=[bg])
        P.act(gts[:], gts[:], AF.Sigmoid, [bg], [bg])
        w1b = [P.tile("w1b%d" % i, [128, 32, 128], BF16) for i in range(2)]
        w2b = [P.tile("w2b%d" % i, [128, 128], BF16) for i in range(2)]
        posb = [P.tile("posb%d" % i, [128, 32], BF16) for i in range(2)]
        bw1 = Buf()
        for i, (w1, w2, pn) in enumerate(((w1k, w2k, "posk"), (w1v, w2v, "posv"))):
            P.dma("pool", w1b[i][:], w1[l].rearrange("s d e -> d s e"), writes=[bw1])
            P.dma("pool", w2b[i][:], w2[l], writes=[bw1])
            P.cp(posb[i][:], CPc(l, pn), [], [bw1])
        ckk = P.tile("ckk", [128, 2], F32); bck = Buf()
        kct = ARB(24576, T); vct = ARB(26624, T); bkv = Buf()
        kst = ARB(28672, T); kwt = ARB(30720, T); bks = Buf()
        vst = P.tile("vst", [128, 16, 128], BF16); vwt = P.tile("vwt", [128, 16, 128], BF16); bvs = Buf()
        qt = [ARB(i * T, T) for i in range(4)]
        bq = [Buf() for _ in range(4)]
        hid = P.tile("hid", [128, 128], BF16); bhid = Buf()
        xh = P.tile("xh", [128, 128], F32); bxh = Buf()
        th_ = P.tile("th_", [128, 128], F32); bth = Buf()
        kcmp = P.tile("kcmp", [128, 128], BF16); vcmp = P.tile("vcmp", [128, 128], BF16); bcmp = Buf()
        oast3 = ARB(8192, 4 * T).rearrange("p (h t) -> p h t", t=T)
        oast = [ARB(8192 + i * T, T) for i in range(4)]
        boast = Buf()
        oacc3 = [P.tile("oacc%d" % i, [128, 4, 128], F32) for i in range(2)]
        boacc = [[Buf() for _ in range(4)] for _ in range(2)]
        obf4 = P.tile("obf4", [128, 4, 128], BF16); bobf = Buf()
        biasv = [AR(10240, T), AR(19712, T)]
        bbias = [Buf(), Buf()]
        biasf = [b_.rearrange("p (a b) -> p a b", b=64) for b_ in biasv]
        smc = AR(21760, 512); bsmc = Buf()
        smc3 = smc.rearrange("p (h n) -> p h n", n=128)
        pbc = ARB(44544, 512); bpbc = Buf()
        pbc3 = pbc.rearrange("p (h n) -> p h n", n=128)
        pTc = ARB(45056, 512).rearrange("p (h n) -> p h n", n=128); bpTc = Buf()
        smlc = P.tile("smlc", [128, 16], F32); bsmlc = Buf()
        sc = P.tile("sc", [128, 32], F32); sc2 = P.tile("sc2", [128, 32], F32); bsc = Buf()
        m8 = P.tile("m8", [128, 16], F32)
        selb = P.tile("selb", [128, 32], F32)
        ps_m = P.tile("ps_m", [128, 512], F32, psum=True); bpm = Buf()
        ps_m3 = ps_m[:].rearrange("p (h n) -> p h n", n=128)
        ps_s = P.tile("ps_s", [128, T], F32, psum=True); bps = Buf()
        ps_w = P.tile("ps_w", [128, 1024], F32, psum=True); bpw = Buf()
        ps_imp = ps_w[:, 0:32]
        ps_t = P.tile("ps_t", [128, 8, 128], BF16, psum=True); bpt = Buf()
        lanes = [
            dict(ps=ps_s, bps=bps, sm=AR(8192, T), bsm=Buf(), pb=ARB(32768, T), bpb=Buf(),
                 pT=ARB(34816, T).rearrange("p (k q) -> p k q", q=128), bpT=Buf(),
                 sml=P.tile("sml_s", [128, 4], F32), bsml=Buf(), tps=ps_t, btps=bpt, tcap=8, pv=ps_m[:, 0:128], bpv=bpm),
            dict(ps=ps_w, bps=bpw, sm=AR(18432, 640), bsm=Buf(), pb=ARB(38144, 640), bpb=Buf(),
                 pT=ARB(38784, 640).rearrange("p (k q) -> p k q", q=128), bpT=Buf(),
                 sml=P.tile("sml_w", [128, 4], F32), bsml=Buf(),
                 tps=ps_w[:, 768:1024].bitcast(BF16).rearrange("p (k q) -> p k q", q=128), btps=bpw, tcap=4,
                 pv=ps_w[:, 640:768], bpv=bpw),
        ]
        blkb = P.tile("blkb", [128, T], BF16)
        winbb = P.tile("winbb", [128, 640], BF16)
        selbb = P.tile("selbb", [128, 128], BF16)
        selbT = [P.tile("selbT%d" % i_, [128, 128], BF16) for i_ in range(2)]
        bmk = Buf()
        bselT = [Buf(), Buf()]
        P.dma("pool", blkb[:], blk_in, writes=[bmk])
        P.cp(winbb[:], C("winb"), [], [bmk])
        P.op("dve", lambda e: e.memset(selbb[:], 0.0), [], [bsc])
        P.op("dve", lambda e: e.memset(hid[:], 0.0), [], [bhid])
        P.op("dve", lambda e: e.memset(kcmp[:], 0.0), [], [bcmp])
        P.op("dve", lambda e: e.memset(vcmp[:], 0.0), [], [bcmp])

        def chain(L_, h, i, nk, is_slc, kT, k0, vT, vb0, gcol):
            par = i % 2
            qs = qt[h][:, i * 128:(i + 1) * 128]
            sml = L_["sml"]
            nch = (nk + 511) // 512
            for c in range(nch):
                w = min(512, nk - c * 512)
                c0 = c * 512
                dst = L_["ps"][:, c0:c0 + w]
                P.mm(dst, qs, kT[:, k0 + c0:k0 + c0 + w], True, False, [bq[h], bks], [L_["bps"]], inc=False)
                if is_slc:
                    dg = (c == nch - 1)
                    P.mm(dst, selbT[par][:], blkb[:, c0:c0 + w], False, not dg, [bselT[par], bmk], [L_["bps"]], inc=False)
                    if dg:
                        P.mm(L_["ps"][:, nk - 128:nk], identb[:], winbb[:, 512:640], False, True, [bmk], [L_["bps"]],
                             inc=True)
                else:
                    off = 640 - nk
                    P.mm(dst, identb[:], winbb[:, off + c0:off + c0 + w], False, True, [bmk], [L_["bps"]],
                         inc=(c == nch - 1))
            yield
            P.op("dve", lambda e: e.reduce_max(out=sml[:, 0:1], in_=L_["ps"][:, 0:nk], axis=AX.X, negate=True),
                 [L_["bps"]], [L_["bsml"]])
            yield
            P.act(L_["pb"][:, 0:nk], L_["ps"][:, 0:nk], AF.Exp, [L_["bps"], L_["bsml"]], [L_["bpb"], L_["bsml"]],
                  bias=sml[:, 0:1], accum_out=sml[:, 1:2])
            yield
            nb = nk // 128
            tcap = L_["tcap"]
            for r0 in range(0, nb, tcap):
                nr = min(tcap, nb - r0)
                for kb in range(nr):
                    P.tr(L_["tps"][:, kb, :], L_["pb"][:, (r0 + kb) * 128:(r0 + kb + 1) * 128], identb[:], [L_["bpb"]],
                         [L_["btps"]], inc=(kb == nr - 1))
                P.cp(L_["pT"][:, r0:r0 + nr, :], L_["tps"][:, 0:nr, :], [L_["btps"]], [L_["bpT"]], eng="act")
                yield
            for kb in range(nb):
                P.mm(L_["pv"], L_["pT"][:, kb, :], vT[:, vb0 + kb, :], kb == 0, kb == nb - 1, [L_["bpT"], bvs], [L_["bpv"]])
            P.op("dve", lambda e: e.reciprocal(out=sml[:, 2:3], in_=sml[:, 1:2]), [L_["bsml"]], [L_["bsml"]])
            P.tt(sml[:, 3:4], sml[:, 2:3], gts[:, i, gcol:gcol + 1], ALU.mult, [L_["bsml"], bg], [L_["bsml"]])
            P.stt(oacc3[par][:, h, :], L_["pv"], sml[:, 3:4], oacc3[par][:, h, :], ALU.mult, ALU.add,
                  [L_["bpv"], L_["bsml"], boacc[par][h]], [boacc[par][h]])
            yield

        def lane_slc(g, i):
            for h in range(4):
                hd = g * 4 + h
                yield from chain(lanes[0], h, i, 128 * (i + 1), True, kst, 0, vst, 0, hd * 3 + 1)

        def lane_win(g, i):
            j0 = max(0, i - 4)
            nkw = (i - j0 + 1) * 128
            for h in range(4):
                hd = g * 4 + h
                yield from chain(lanes[1], h, i, nkw, False, kwt, j0 * 128, vwt, j0, hd * 3 + 2)

        for g in range(4):
            P.dma("sp", kct[:], kcT[g * 128:(g + 1) * 128, :], writes=[bkv])
            P.dma("sp", vct[:], vcT[g * 128:(g + 1) * 128, :], writes=[bkv])
            P.dma("sp", kst[:], ksT[g * 128:(g + 1) * 128, :], writes=[bks])
            P.dma("sp", kwt[:], kwT[g * 128:(g + 1) * 128, :], writes=[bks])
            P.dma("sp", vst[:], vsd[:, g * 128:(g + 1) * 128].rearrange("(tt p) d -> p tt d", p=128), writes=[bvs])
            P.dma("sp", vwt[:], vwd[:, g * 128:(g + 1) * 128].rearrange("(tt p) d -> p tt d", p=128), writes=[bvs])
            for h in range(4):
                hd = g * 4 + h
                P.dma("sp", qt[h][:], qT[hd * 128:(hd + 1) * 128, :], writes=[bq[h]])
            for i, src in enumerate((kct, vct)):
                for s_ in range(32):
                    P.mm(ps_m[:, 256 + i:257 + i], w1b[i][:, s_, :], posb[i][:, s_:s_ + 1], s_ == 0, s_ == 31, [bw1], [bpm])
                P.cp(ckk[:, i:i + 1], ps_m[:, 256 + i:257 + i], [bpm], [bck])
                for s_ in range(32):
                    P.mm(ps_m[:, 0:127], w1b[i][:, s_, :], src[:, s_:s_ + 2017:16], s_ == 0, s_ == 31, [bw1, bkv], [bpm])
                P.act(xh[:, 0:127], ps_m[:, 0:127], AF.Identity, [bpm, bck], [bxh], bias=ckk[:, i:i + 1])
                P.gelu(hid[:, 0:127], xh[:, 0:127], th_[:, 0:127], [bxh], bth, [bhid])
                if i == 0:
                    P.mm(ps_m[:, 128:255], w2b[0][:], hid[:, 0:127], True, True, [bw1, bhid], [bpm])
                    P.cp(kcmp[:, 0:127], ps_m[:, 128:255], [bpm], [bcmp])
                else:
                    P.mm(ps_m[0:127, 128:256], hid[:, 0:127], w2b[1][:], True, True, [bw1, bhid], [bpm])
                    P.cp(vcmp[0:127, :], ps_m[0:127, 128:256], [bpm], [bcmp])
            for i in range(16):
                par = i % 2
                qc = slice(i * 128, (i + 1) * 128)
                for h in range(4):
                    P.mm(ps_m[:, h * 128:(h + 1) * 128], qt[h][:, qc], kcmp[:], True, True, [bq[h], bcmp], [bpm], inc=(h == 3))
                P.tt(smc3, ps_m3, C("cmpb", i * 128, (i + 1) * 128).unsqueeze(1).broadcast_to([128, 4, 128]), ALU.add,
                     [bpm], [bsmc])
                P.op("dve", lambda e: e.tensor_reduce(out=smlc[:, 0:4], in_=smc3, axis=AX.X, op=ALU.max), [bsmc], [bsmlc])
                P.tt(smc3, smc3, smlc[:, 0:4].unsqueeze(2).broadcast_to([128, 4, 128]), ALU.subtract, [bsmc, bsmlc], [bsmc])
                P.act(smc, smc, AF.Exp, [bsmc], [bsmc])
                P.op("dve", lambda e: e.tensor_reduce(out=smlc[:, 4:8], in_=smc3, axis=AX.X, op=ALU.add), [bsmc], [bsmlc])
                P.ts(smlc[:, 4:8], smlc[:, 4:8], 1e-30, None, ALU.max, ALU.bypass, [bsmlc], [bsmlc])
                P.op("dve", lambda e: e.reciprocal(out=smlc[:, 8:12], in_=smlc[:, 4:8]), [bsmlc], [bsmlc])
                P.ts(smlc[:, 8:12], smlc[:, 8:12], C("rowv", i, i + 1), None, ALU.mult, ALU.bypass, [bsmlc], [bsmlc])
                P.tt(pbc3, smc3, smlc[:, 8:12].unsqueeze(2).broadcast_to([128, 4, 128]), ALU.mult, [bsmc, bsmlc], [bpbc])
                for h in range(4):
                    P.tr(ps_t[:, h, :], pbc[:, h * 128:(h + 1) * 128], identb[:], [bpbc], [bpt], inc=(h == 3))
                P.cp(pTc[:, 0:4, :], ps_t[:, 0:4, :], [bpt], [bpTc], eng="act")
                for h in range(4):
                    P.mm(ps_m[:, h * 128:(h + 1) * 128], pTc[0:127, h, :], vcmp[0:127, :], True, True, [bpTc, bcmp], [bpm],
                         inc=(h == 3))
                for h in range(4):
                    P.mm(ps_imp, pTc[0:127, h, :], ovlb[0:127, :], h == 0, h == 3, [bpTc], [bpw])
                P.tt(oacc3[par][:], ps_m3, gts[:, i, g * 12:(g + 1) * 12:3].unsqueeze(2).broadcast_to([128, 4, 128]), ALU.mult,
                     [bpm, bg], boacc[par])
                P.tt(sc[:], ps_imp, C("validm", i * 32, (i + 1) * 32), ALU.mult, [bpw], [bsc])
                P.tt(sc[:], sc[:], C("addc", i * 32, (i + 1) * 32), ALU.add, [bsc], [bsc])
                P.op("dve", lambda e: e.max(out=m8[:, 0:8], in_=sc[:]), [bsc], [bsc])
                P.op("dve", lambda e: e.match_replace(out=sc2[:], in_to_replace=m8[:, 0:8], in_values=sc[:], imm_value=-1e9), [bsc], [bsc])
                P.op("dve", lambda e: e.max(out=m8[:, 8:16], in_=sc2[:]), [bsc], [bsc])
                P.ts(selbb[:, 0:32], sc[:], m8[:, 15:16], NEG, ALU.is_lt, ALU.mult, [bsc], [bsc])
                P.tr(ps_t[:, 7, :], selbb[:], identb[:], [bsc], [bpt])
                P.cp(selbT[par][:], ps_t[:, 7, :], [bpt], [bselT[par]], eng="act")
                active = [lane_slc(g, i), lane_win(g, i)]
                if os.environ.get("KLANES") == "seq":
                    for gen_ in active:
                        for _ in gen_:
                            pass
                    active = []
                while active:
                    for gen_ in list(active):
                        try:
                            next(gen_)
                        except StopIteration:
                            active.remove(gen_)
                P.cp(obf4[:], oacc3[par][:], boacc[par], [bobf], eng="pool")
                for h in range(4):
                    P.tr(ps_t[:, h, :], obf4[:, h, :], identb[:], [bobf], [bpt], inc=(h == 3))
                P.cp(oast3[:, :, qc], ps_t[:, 0:4, :], [bpt], [boast], eng="act")
            for h in range(4):
                hd = g * 4 + h
                P.dma("sp", oaT[hd * 128:(hd + 1) * 128, :], oast[h][:], reads=[boast])
        P.phase_end()

        P.phase_begin()
        ws = WStream()
        HT = 1024
        oah = arena[:, 0:16384].rearrange("p (k t) -> p k t", t=HT)
        obh = arena[:, 16384:32768].rearrange("p (k t) -> p k t", t=HT)
        mth = arena[:, 32768:49152].rearrange("p (k t) -> p k t", t=HT)
        boah, bobh, bmth = Buf(), Buf(), Buf()
        psy = [P.tile("psy%d" % i, [128, HT], F32, psum=True) for i in range(4)]
        bpsy = [Buf() for _ in range(4)]
        sgt = [P.tile("sgt%d" % i, [128, 2, HT], BF16) for i in range(2)]
        bsgt = [Buf() for _ in range(2)]
        m1 = [P.tile("m1_%d" % i, [128, HT], F32) for i in range(2)]
        bm1 = [Buf() for _ in range(2)]
        xo_ = [P.tile("xold%d" % i, [128, HT], F32) for i in range(2)]
        bxo_ = [Buf() for _ in range(2)]
        for th in range(2):
            t0 = th * HT
            for k4 in range(4):
                P.dma("sp", oah[:, k4 * 4:(k4 + 1) * 4, :],
                      oaT[k4 * 512:(k4 + 1) * 512, t0:t0 + HT].rearrange("(k p) t -> p k t", p=128), writes=[boah])
                P.dma("sp", obh[:, k4 * 4:(k4 + 1) * 4, :],
                      obT[k4 * 512:(k4 + 1) * 512, t0:t0 + HT].rearrange("(k p) t -> p k t", p=128), writes=[bobh])
            fi = 0
            for c4 in range(4):
                wa_, bwa = ws.load(proj_a[l][:, c4 * 512:(c4 + 1) * 512], KC, 512)
                wb_, bwb = ws.load(proj_b[l][:, c4 * 512:(c4 + 1) * 512], KC, 512)
                for m in range(4):
                    f = c4 * 4 + m
                    s = fi % 2
                    fi += 1
                    pa, pbb = 2 * s, 2 * s + 1
                    P.dma("sp", sgt[s][:, 0, :], sgaT[f * 128:(f + 1) * 128, t0:t0 + HT], writes=[bsgt[s]])
                    P.dma("sp", sgt[s][:, 1, :], sgbT[f * 128:(f + 1) * 128, t0:t0 + HT], writes=[bsgt[s]])
                    for (pp, wv, bw, src, bsrc) in ((pa, wa_, bwa, oah, boah), (pbb, wb_, bwb, obh, bobh)):
                        for k in range(KC):
                            for n in range(2):
                                P.mm(psy[pp][:, n * 512:(n + 1) * 512], wv[:, k, m * 128:(m + 1) * 128],
                                     src[:, k, n * 512:(n + 1) * 512], k == 0, k == KC - 1, [bw, bsrc], [bpsy[pp]],
                                     inc=(k == KC - 1 and n == 1))
                    P.tt(m1[s][:], psy[pa][:], sgt[s][:, 0, :], ALU.mult, [bpsy[pa], bsgt[s]], [bm1[s]])
                    P.tt(sgt[s][:, 1, :], psy[pbb][:], sgt[s][:, 1, :], ALU.mult, [bpsy[pbb], bsgt[s]], [bsgt[s]])
                    P.tt(mth[:, f, :], m1[s][:], sgt[s][:, 1, :], ALU.add, [bm1[s], bsgt[s]], [bmth], eng="pool")
            fi = 0
            for c4 in range(4):
                wo_, bwo = ws.load(w_out[l][:, c4 * 512:(c4 + 1) * 512], KC, 512)
                for m in range(4):
                    f = c4 * 4 + m
                    pp = fi % 4
                    s = fi % 2
                    fi += 1
                    P.dma("sp", xo_[s][:], xT[f * 128:(f + 1) * 128, t0:t0 + HT], writes=[bxo_[s]])
                    for k in range(KC):
                        for n in range(2):
                            P.mm(psy[pp][:, n * 512:(n + 1) * 512], wo_[:, k, m * 128:(m + 1) * 128],
                                 mth[:, k, n * 512:(n + 1) * 512], k == 0, k == KC - 1, [bwo, bmth], [bpsy[pp]],
                                 inc=(k == KC - 1 and n == 1))
                    P.stt(xo_[s][:], psy[pp][:], g1c[:, f:f + 1], xo_[s][:], ALU.mult, ALU.add, [bpsy[pp], bxo_[s]], [bxo_[s]])
                    P.dma("sp", xT[f * 128:(f + 1) * 128, t0:t0 + HT], xo_[s][:], reads=[bxo_[s]])
        P.phase_end()

        norm_phase(A2[:, l, :], sh2)

        P.phase_begin()
        ws = WStream()
        psg = [P.tile("psg%d" % i, [128, 1024], F32, psum=True) for i in range(4)]
        bpsg = [Buf() for _ in range(4)]
        gp = [P.tile("gp%d" % i, [128, T + 2], F32) for i in range(2)]
        bgp = [Buf() for _ in range(2)]
        vv = [AR(16384 + i * T, T) for i in range(2)]
        bvv = [Buf() for _ in range(2)]
        tmpg = AR(16384 + 2 * T, T); btmpg = Buf()
        gc = P.tile("gc", [128, T], F32); bgc = Buf()
        hb = [P.tile("hb%d" % i, [128, T], BF16) for i in range(2)]
        bhb = [Buf() for _ in range(2)]
        for i in range(2):
            P.op("pool", lambda e, i=i: e.memset(gp[i][:, 0:2], 0.0), [], [bgp[i]])
        for c4 in range(12):
            wg_, bwg_ = ws.load(ffn_up[l][:, c4 * 512:(c4 + 1) * 512], KC, 512)
            wv_, bwv_ = ws.load(ffn_up[l][:, DFF + c4 * 512:DFF + (c4 + 1) * 512], KC, 512)
            for m in range(4):
                j = c4 * 4 + m
                s = j % 2
                for hf in range(2):
                    for (pp, wv, bw) in ((2 * hf, wg_, bwg_), (2 * hf + 1, wv_, bwv_)):
                        for k in range(KC):
                            for n in range(2):
                                tn = hf * 1024 + n * 512
                                P.mm(psg[pp][:, n * 512:(n + 1) * 512], wv[:, k, m * 128:(m + 1) * 128],
                                     hT[:, k, tn:tn + 512], k == 0, k == KC - 1, [bw, bH[k]], [bpsg[pp]],
                                     inc=(k == KC - 1 and n == 1))
                    P.cp(gp[s][:, 2 + hf * 1024:2 + (hf + 1) * 1024], psg[2 * hf][:], [bpsg[2 * hf]], [bgp[s]], eng="act")
                    P.cp(vv[s][:, hf * 1024:(hf + 1) * 1024], psg[2 * hf + 1][:], [bpsg[2 * hf + 1]], [bvv[s]], eng="act")
                fw_ = CPc(l, "fcw", j * 3, j * 3 + 3)
                P.ts(gc[:], gp[s][:, 2:T + 2], fw_[:, 2:3], CPc(l, "fcb", j, j + 1), ALU.mult, ALU.add, [bgp[s]], [bgc])
                P.stt(gc[:], gp[s][:, 1:T + 1], fw_[:, 1:2], gc[:], ALU.mult, ALU.add, [bgp[s], bgc], [bgc])
                P.stt(gc[:], gp[s][:, 0:T], fw_[:, 0:1], gc[:], ALU.mult, ALU.add, [bgp[s], bgc], [bgc])
                P.gelu(None, gc[:], tmpg, [bgc], btmpg, None)
                P.tt(tmpg, tmpg, gc[:], ALU.mult, [btmpg, bgc], [btmpg])
                P.tt(hb[s][:], tmpg, vv[s], ALU.mult, [btmpg, bvv[s]], [bhb[s]], eng="pool")
                P.dma("sp", hidT[j * 128:(j + 1) * 128, :], hb[s][:], reads=[bhb[s]])
        P.phase_end()

        P.phase_begin()
        ws = WStream()
        hdh = arena[:, 0:49152].rearrange("p (k t) -> p k t", t=HT)
        bhd = [Buf() for _ in range(6)]
        psd = [P.tile("psd%d" % i, [128, HT], F32, psum=True) for i in range(4)]
        bpsd = [Buf() for _ in range(4)]
        xo2 = [P.tile("xold2_%d" % i, [128, HT], F32) for i in range(2)]
        bxo2 = [Buf() for _ in range(2)]
        for th in range(2):
            t0 = th * HT
            for k8 in range(6):
                P.dma("sp", hdh[:, k8 * 8:(k8 + 1) * 8, :],
                      hidT[k8 * 1024:(k8 + 1) * 1024, t0:t0 + HT].rearrange("(k p) t -> p k t", p=128), writes=[bhd[k8]])
            fi = 0
            for c2 in range(16):
                wd_, bwd = ws.load(ffn_down[l][:, c2 * 128:(c2 + 1) * 128], 48, 128)
                for m in range(1):
                    f = c2
                    pp = fi % 4
                    s = fi % 2
                    fi += 1
                    P.dma("sp", xo2[s][:], xT[f * 128:(f + 1) * 128, t0:t0 + HT], writes=[bxo2[s]])
                    for k in range(48):
                        for n in range(2):
                            P.mm(psd[pp][:, n * 512:(n + 1) * 512], wd_[:, k, m * 128:(m + 1) * 128],
                                 hdh[:, k, n * 512:(n + 1) * 512], k == 0, k == 47, [bwd, bhd[k // 8]], [bpsd[pp]],
                                 inc=(k == 47 and n == 1))
                    P.stt(xo2[s][:], psd[pp][:], g2c[:, f:f + 1], xo2[s][:], ALU.mult, ALU.add, [bpsd[pp], bxo2[s]], [bxo2[s]])
                    P.dma("sp", xT[f * 128:(f + 1) * 128, t0:t0 + HT], xo2[s][:], reads=[bxo2[s]])
        P.phase_end()

    fg = C("fing")
    norm_phase(fg, None, final=True)
    P.finish()
    return nc, P


def _consts():
    q = np.arange(128)
    cst = np.zeros((128, NCS), np.float32)

    def put(name, arr):
        o, w = CS[name]
        cst[:, o:o + w] = arr.reshape(128, w)
    t = (np.arange(16)[None, :] * 128 + q[:, None])
    put("rowv", (t >= 31).astype(np.float32))
    n = np.arange(128)
    cm = (16 * n[None, None, :] + 31 <= t[:, :, None]) & (n[None, None, :] < 127)
    put("cmpb", np.where(cm, 0.0, NEG).astype(np.float32))
    wb = np.zeros((128, 640), np.float32)
    kk = np.arange(128)
    wb[:, 0:128] = np.where(kk[None, :] > q[:, None], 0.0, NEG)
    wb[:, 512:640] = np.where(kk[None, :] <= q[:, None], 0.0, NEG)
    put("winb", wb)
    blk = np.arange(32)[None, None, :]
    cur = (t // 64)[:, :, None]
    valid = blk <= cur
    forced = (blk == 0) | (valid & (blk > cur - 2))
    put("validm", valid.astype(np.float32))
    put("addc", np.where(forced, 1e4, np.where(valid, 0.0, -1.0)).astype(np.float32))
    cs = np.arange(128) * 16
    ss = np.arange(32) * 64
    ov = ((cs[:, None] < ss[None, :] + 64) & (cs[:, None] + 32 > ss[None, :])).astype(np.float32)
    ov[127, :] = 0.0
    put("ovl", ov)
    put("ident", np.eye(128, dtype=np.float32))
    return cst


def _col(v, n):
    return np.swapaxes(v.reshape(v.shape[:-1] + (n, 128)), -1, -2)


_CACHE = {}


def kernel(x, c, ada_w, ada_b, norm1_g, w_in, cmp_pos_k, cmp_w1_k, cmp_w2_k, cmp_pos_v, cmp_w1_v, cmp_w2_v,
           lru_conv_w, lru_conv_b, lru_wa, lru_ba, lru_wi, lru_bi, lru_lambda, proj_a, proj_b, w_out, norm2_g,
           ffn_up, ffn_conv_w, ffn_conv_b, ffn_down, final_g, _cores=None, _dbg=False, _stop=None):
    f = lambda a: np.ascontiguousarray(np.asarray(a, dtype=np.float32))
    x = f(x)
    L = int(np.asarray(w_in).shape[0])
    B = x.shape[0]
    cores = list(range(B)) if _cores is None else list(_cores)
    colp = np.zeros((L, 128, NCP), np.float32)

    def putc(name, arr):
        o, w = CP[name]
        colp[:, :, o:o + w] = arr.reshape(L, 128, w)
    putc("adab", _col(f(ada_b), 96))
    putc("n1g", _col(f(norm1_g), 16))
    putc("n2g", _col(f(norm2_g), 16))
    putc("lcw", np.transpose(f(lru_conv_w).reshape(L, 4, 16, 128), (0, 3, 2, 1)))
    putc("lcb", _col(f(lru_conv_b), 16))
    putc("lba", _col(f(lru_ba), 16))
    putc("lbi", _col(f(lru_bi), 16))
    putc("llam", _col(f(lru_lambda), 16))
    putc("fcw", np.transpose(f(ffn_conv_w).reshape(L, 3, 48, 128), (0, 3, 2, 1)))
    putc("fcb", _col(f(ffn_conv_b), 48))
    putc("posk", np.transpose(f(cmp_pos_k), (0, 2, 1)))
    putc("posv", np.transpose(f(cmp_pos_v), (0, 2, 1)))
    cst0 = _consts()
    fg = _col(f(final_g), 16)
    shared = {"colp": colp, "ada_w": f(ada_w), "w_in": f(w_in), "cmp_w1_k": f(cmp_w1_k), "cmp_w2_k": f(cmp_w2_k),
              "cmp_w1_v": f(cmp_w1_v), "cmp_w2_v": f(cmp_w2_v), "lru_wa": f(lru_wa), "lru_wi": f(lru_wi),
              "proj_a": f(proj_a), "proj_b": f(proj_b), "w_out": f(w_out), "ffn_up": f(ffn_up), "ffn_down": f(ffn_down)}
    blk = np.zeros((128, T), np.float32)
    blk[np.arange(T) // 64, np.arange(T)] = 1.0
    in_maps = []
    cc = f(c)
    for b in cores:
        cst = cst0.copy()
        o, w = CS["cT"]
        cst[:, o:o + w] = _col(cc[b], 16)
        o, w = CS["fing"]
        cst[:, o:o + w] = fg
        m = dict(shared)
        m["x"] = x[b]
        m["cst"] = cst
        m["blkind"] = blk
        in_maps.append(m)
    key = (L, _dbg, _stop)
    if key not in _CACHE:
        _CACHE[key] = build(L, _dbg, _stop)[0]
    nc = _CACHE[key]
    res = run_bass_kernel_spmd(nc, in_maps, core_ids=list(range(len(cores))))
    if _dbg:
        return res.results
    return np.stack([np.asarray(r["out"], dtype=np.float32) for r in res.results], axis=0)
```
